# Optimizing a Trainium2 kernel written in Bass

```python
import math
import jax, jax.numpy as jnp
from jax import lax
import numpy as np

D_MODEL = 1024
BATCH = 4
SEQ = 4096
DEPTH = 4

PLE_DIM = 256
HEAD_DIM = 64
ROPE_DIM = HEAD_DIM // 4
ROPE_THETA = 500000.0
NORM_EPS = 1e-6
NEG_INF = -1e30

MOBA_HEADS = 8
MOBA_BLOCK = 256
MOBA_TOPK = 3
MOBA_QCHUNK = 64
MOBA_WIDTH = MOBA_HEADS * HEAD_DIM

DIL_WINDOWS = (128, 512, 2048)
DIL_RATES = (1, 4, 16)
DIL_GROUPS = 3
DIL_HEADS = 8
DIL_NKEYS = DIL_WINDOWS[0] // DIL_RATES[0] + 1
DIL_QBLOCK = 64
DIL_WIDTH = DIL_HEADS * HEAD_DIM

SSM_INNER = D_MODEL
SSM_HEAD_DIM = 64
SSM_HEADS = SSM_INNER // SSM_HEAD_DIM
SSM_GROUPS = 4
SSM_STATE = 128
SSM_CONV = 4
SSM_CHUNK = 128
SSM_XBC = SSM_INNER + 2 * SSM_GROUPS * SSM_STATE

FFN_DIM = 2816
FFN_CONV = 3

N_BRANCH = 3
IN_SIZES = (3 * MOBA_WIDTH, 3 * DIL_GROUPS * DIL_WIDTH, SSM_INNER, SSM_XBC, SSM_HEADS, N_BRANCH * D_MODEL)
IN_COLS = sum(IN_SIZES)

kernel_name = "hybrid_moba_ssd_dilated_block"


def rms_norm(x, g):
    xf = x.astype(jnp.float32)
    y = xf * lax.rsqrt(jnp.mean(xf * xf, axis=-1, keepdims=True) + NORM_EPS)
    return (y * g.astype(jnp.float32)).astype(x.dtype)


def rope_tables(positions):
    inv = ROPE_THETA ** (-jnp.arange(0, ROPE_DIM, 2, dtype=jnp.float32) / ROPE_DIM)
    ang = positions.astype(jnp.float32)[..., None] * inv
    return jnp.cos(ang), jnp.sin(ang)


def apply_rope(x, cos, sin):
    half = ROPE_DIM // 2
    bshape = cos.shape[:2] + (1,) * (x.ndim - 3) + (half,)
    c = cos.reshape(bshape).astype(x.dtype)
    s = sin.reshape(bshape).astype(x.dtype)
    x1 = x[..., :half]
    x2 = x[..., half:ROPE_DIM]
    return jnp.concatenate([x1 * c - x2 * s, x2 * c + x1 * s, x[..., ROPE_DIM:]], axis=-1)


def causal_dwconv(x, w, b):
    k_width, chans = w.shape
    y = lax.conv_general_dilated(
        x, w.astype(x.dtype)[:, None, :], window_strides=(1,), padding=((k_width - 1, 0),),
        dimension_numbers=('NWC', 'WIO', 'NWC'), feature_group_count=chans)
    return y + b.astype(x.dtype)


def moba_attention(q, k, v):
    bsz, S, H, dh = q.shape
    scale = dh ** -0.5
    qt, kt, vt = (t.transpose(0, 2, 1, 3) for t in (q, k, v))
    nb = -(-S // MOBA_BLOCK)
    pad = nb * MOBA_BLOCK - S
    kp = jnp.pad(kt, ((0, 0), (0, 0), (0, pad), (0, 0)))
    vp = jnp.pad(vt, ((0, 0), (0, 0), (0, pad), (0, 0)))
    kb = kp.reshape(bsz, H, nb, MOBA_BLOCK, dh)
    vb = vp.reshape(bsz, H, nb, MOBA_BLOCK, dh)
    kmean = kb.astype(jnp.float32).mean(axis=3)
    topk = min(MOBA_TOPK, nb)
    blk_ids = jnp.arange(nb)
    gather = jax.vmap(jax.vmap(lambda t, i: t[i]))

    def one_chunk(ci):
        t0 = ci * MOBA_QCHUNK
        jb = t0 // MOBA_BLOCK
        qpos = t0 + jnp.arange(MOBA_QCHUNK)
        qc = lax.dynamic_slice_in_dim(qt, t0, MOBA_QCHUNK, axis=2)
        score = jnp.einsum('bhqd,bhnd->bhqn', qc.astype(jnp.float32), kmean)
        score = jnp.where(blk_ids < jb, score, -jnp.inf)
        _, sel = lax.top_k(score, topk)
        sel_ok = sel < jb
        ks = gather(kb, sel)
        vs = gather(vb, sel)
        l_sel = jnp.einsum('bhqd,bhqtkd->bhqtk', qc, ks).astype(jnp.float32) * scale
        l_sel = jnp.where(sel_ok[..., None], l_sel, NEG_INF)
        l_sel = l_sel.reshape(bsz, H, MOBA_QCHUNK, topk * MOBA_BLOCK)
        k_own = lax.dynamic_slice_in_dim(kp, jb * MOBA_BLOCK, MOBA_BLOCK, axis=2)
        v_own = lax.dynamic_slice_in_dim(vp, jb * MOBA_BLOCK, MOBA_BLOCK, axis=2)
        l_own = jnp.einsum('bhqd,bhkd->bhqk', qc, k_own).astype(jnp.float32) * scale
        kpos = jb * MOBA_BLOCK + jnp.arange(MOBA_BLOCK)
        l_own = jnp.where(kpos[None, :] <= qpos[:, None], l_own, NEG_INF)
        w = jax.nn.softmax(jnp.concatenate([l_own, l_sel], axis=-1), axis=-1).astype(v.dtype)
        w_own = w[..., :MOBA_BLOCK]
        w_sel = w[..., MOBA_BLOCK:].reshape(bsz, H, MOBA_QCHUNK, topk, MOBA_BLOCK)
        return (jnp.einsum('bhqk,bhkd->bhqd', w_own, v_own)
                + jnp.einsum('bhqtk,bhqtkd->bhqd', w_sel, vs))

    out = lax.map(one_chunk, jnp.arange(S // MOBA_QCHUNK))
    out = jnp.moveaxis(out, 0, 2).reshape(bsz, H, S, dh)
    return out.transpose(0, 2, 1, 3).reshape(bsz, S, H * dh)


def dilated_attention(q, k, v):
    bsz, S, G, hd, dh = q.shape
    scale = dh ** -0.5
    qt, kt, vt = (t.transpose(0, 2, 3, 1, 4) for t in (q, k, v))
    rates = jnp.array(DIL_RATES, dtype=jnp.int32)
    offs = jnp.arange(DIL_NKEYS, dtype=jnp.int32)
    gather = jax.vmap(lambda t, i: jnp.take(t, i, axis=2), in_axes=(1, 0), out_axes=1)

    def one_block(bi):
        t0 = bi * DIL_QBLOCK
        qpos = t0 + jnp.arange(DIL_QBLOCK)
        idx = qpos[None, :, None] - rates[:, None, None] * offs[None, None, :]
        ok = idx >= 0
        idx = jnp.maximum(idx, 0)
        qb = lax.dynamic_slice_in_dim(qt, t0, DIL_QBLOCK, axis=3)
        kg = gather(kt, idx)
        vg = gather(vt, idx)
        l = jnp.einsum('bghqd,bghqkd->bghqk', qb, kg).astype(jnp.float32) * scale
        l = jnp.where(ok[None, :, None], l, NEG_INF)
        m = l.max(axis=-1, keepdims=True)
        e = jnp.exp(l - m)
        den = e.sum(axis=-1, keepdims=True)
        o_g = jnp.einsum('bghqk,bghqkd->bghqd', (e / den).astype(v.dtype), vg)
        lse = (m + jnp.log(den))[..., 0]
        alpha = jax.nn.softmax(lse, axis=1).astype(v.dtype)
        return jnp.einsum('bghq,bghqd->bhqd', alpha, o_g)

    out = lax.map(one_block, jnp.arange(S // DIL_QBLOCK))
    out = jnp.moveaxis(out, 0, 2).reshape(bsz, hd, S, dh)
    return out.transpose(0, 2, 1, 3).reshape(bsz, S, hd * dh)


def ssd_scan(x, dt, a, bm, cm):
    bsz, S, H, P = x.shape
    G, N = bm.shape[2], bm.shape[3]
    hg = H // G
    L = SSM_CHUNK
    nc = S // L
    xdt = (x.astype(jnp.float32) * dt[..., None]).reshape(bsz, nc, L, G, hg, P)
    adt = (dt * a).reshape(bsz, nc, L, G, hg).transpose(0, 3, 4, 1, 2)
    acs = jnp.cumsum(adt, axis=-1)
    bc = bm.astype(jnp.float32).reshape(bsz, nc, L, G, N)
    cc = cm.astype(jnp.float32).reshape(bsz, nc, L, G, N)
    causal = jnp.tril(jnp.ones((L, L), dtype=bool))
    decay = jnp.exp(jnp.where(causal, acs[..., :, None] - acs[..., None, :], -jnp.inf))
    cb = jnp.einsum('bclgn,bcsgn->bgcls', cc, bc)
    y_diag = jnp.einsum('bgcls,bghcls,bcsghp->bclghp', cb, decay, xdt)
    decay_states = jnp.exp(acs[..., -1:] - acs)
    states = jnp.einsum('bclgn,bghcl,bclghp->bcghpn', bc, decay_states, xdt)
    chunk_decay = jnp.exp(acs[..., -1])

    def step(h, inp):
        st, dc = inp
        return dc[..., None, None] * h + st, h

    h0 = jnp.zeros((bsz, G, hg, P, N), jnp.float32)
    _, prev = lax.scan(step, h0, (jnp.moveaxis(states, 1, 0), jnp.moveaxis(chunk_decay, -1, 0)))
    prev = jnp.moveaxis(prev, 0, 1)
    y_off = jnp.einsum('bclgn,bcghpn,bghcl->bclghp', cc, prev, jnp.exp(acs))
    return (y_diag + y_off).reshape(bsz, S, H, P)


def mamba2_mixer(z, xbc, dt_raw, conv_w, conv_b, dt_bias, a_log, d_skip, out_norm):
    bsz, S, _ = z.shape
    xbc = jax.nn.silu(causal_dwconv(xbc, conv_w, conv_b))
    xs, bm, cm = jnp.split(xbc, [SSM_INNER, SSM_INNER + SSM_GROUPS * SSM_STATE], axis=-1)
    dt = jax.nn.softplus(dt_raw.astype(jnp.float32) + dt_bias.astype(jnp.float32))
    a = -jnp.exp(a_log.astype(jnp.float32))
    xh = xs.reshape(bsz, S, SSM_HEADS, SSM_HEAD_DIM)
    y = ssd_scan(xh, dt, a,
                 bm.reshape(bsz, S, SSM_GROUPS, SSM_STATE), cm.reshape(bsz, S, SSM_GROUPS, SSM_STATE))
    y = y + xh.astype(jnp.float32) * d_skip.astype(jnp.float32)[None, None, :, None]
    y = y.reshape(bsz, S, SSM_INNER).astype(z.dtype)
    return rms_norm(y * jax.nn.silu(z), out_norm)


def setup_inputs(seed: int = 0) -> dict:
    key = jax.random.key(seed)
    ks = jax.random.split(key, 32)
    f32 = jnp.float32

    def nrm(k, shape, scale):
        return jax.random.normal(k, shape, f32) * scale

    res_scale = (2 * DEPTH) ** -0.5
    x = nrm(ks[0], (BATCH, SEQ, D_MODEL), 1.0)
    p = nrm(ks[1], (DEPTH, BATCH, SEQ, PLE_DIM), 1.0)
    offset = jax.random.randint(ks[2], (BATCH, 1), 0, 1024, dtype=jnp.int32)
    positions = (offset + jnp.arange(SEQ, dtype=jnp.int32)[None, :]).astype(jnp.int32)
    dt0 = jnp.exp(jax.random.uniform(ks[12], (DEPTH, SSM_HEADS), f32)
                  * (math.log(0.1) - math.log(0.001)) + math.log(0.001))
    return {
        "x": x,
        "p": p,
        "positions": positions,
        "norm_mix": 1.0 + nrm(ks[3], (DEPTH, D_MODEL), 0.1),
        "w_in": nrm(ks[4], (DEPTH, D_MODEL, IN_COLS), D_MODEL ** -0.5),
        "b_gate": nrm(ks[5], (DEPTH, N_BRANCH * D_MODEL), 0.1),
        "moba_q_norm": 1.0 + nrm(ks[6], (DEPTH, HEAD_DIM), 0.1),
        "moba_k_norm": 1.0 + nrm(ks[7], (DEPTH, HEAD_DIM), 0.1),
        "dil_q_norm": 1.0 + nrm(ks[8], (DEPTH, HEAD_DIM), 0.1),
        "dil_k_norm": 1.0 + nrm(ks[9], (DEPTH, HEAD_DIM), 0.1),
        "ssm_conv_w": nrm(ks[10], (DEPTH, SSM_CONV, SSM_XBC), SSM_CONV ** -0.5),
        "ssm_conv_b": nrm(ks[11], (DEPTH, SSM_XBC), 0.1),
        "ssm_dt_bias": dt0 + jnp.log(-jnp.expm1(-dt0)),
        "ssm_a_log": jnp.log(jax.random.uniform(ks[13], (DEPTH, SSM_HEADS), f32, 1.0, 16.0)),
        "ssm_d": 1.0 + nrm(ks[14], (DEPTH, SSM_HEADS), 0.1),
        "ssm_out_norm": 1.0 + nrm(ks[15], (DEPTH, SSM_INNER), 0.1),
        "w_br_moba": nrm(ks[16], (DEPTH, MOBA_WIDTH, D_MODEL), MOBA_WIDTH ** -0.5),
        "w_br_ssm": nrm(ks[17], (DEPTH, SSM_INNER, D_MODEL), SSM_INNER ** -0.5),
        "w_br_dil": nrm(ks[18], (DEPTH, DIL_WIDTH, D_MODEL), DIL_WIDTH ** -0.5),
        "w_out": nrm(ks[19], (DEPTH, D_MODEL, D_MODEL), D_MODEL ** -0.5 * res_scale),
        "norm_ffn": 1.0 + nrm(ks[20], (DEPTH, D_MODEL), 0.1),
        "w_up": nrm(ks[21], (DEPTH, D_MODEL, 2 * FFN_DIM), D_MODEL ** -0.5),
        "ffn_conv_w": nrm(ks[22], (DEPTH, FFN_CONV, 2 * FFN_DIM), FFN_CONV ** -0.5),
        "ffn_conv_b": nrm(ks[23], (DEPTH, 2 * FFN_DIM), 0.1),
        "w_down": nrm(ks[24], (DEPTH, FFN_DIM, D_MODEL), FFN_DIM ** -0.5 * res_scale),
        "norm_ple": 1.0 + nrm(ks[25], (DEPTH, D_MODEL), 0.1),
        "w_ple_gate": nrm(ks[26], (DEPTH, D_MODEL, D_MODEL), D_MODEL ** -0.5),
        "w_ple": nrm(ks[27], (DEPTH, PLE_DIM, D_MODEL), PLE_DIM ** -0.5 * res_scale),
    }


def reference(x, p, positions, norm_mix, w_in, b_gate, moba_q_norm, moba_k_norm, dil_q_norm,
              dil_k_norm, ssm_conv_w, ssm_conv_b, ssm_dt_bias, ssm_a_log, ssm_d, ssm_out_norm,
              w_br_moba, w_br_ssm, w_br_dil, w_out, norm_ffn, w_up, ffn_conv_w, ffn_conv_b,
              w_down, norm_ple, w_ple_gate, w_ple):
    bsz, S, _ = x.shape
    cos, sin = rope_tables(positions)
    split_pts = [int(v) for v in np.cumsum(IN_SIZES)[:-1]]
    for i in range(DEPTH):
        u = rms_norm(x, norm_mix[i])
        proj = u @ w_in[i]
        moba_qkv, dil_qkv, ssm_z, ssm_xbc, ssm_dt, gate_logits = jnp.split(proj, split_pts, axis=-1)

        mq, mk, mv = jnp.split(moba_qkv.reshape(bsz, S, 3, MOBA_HEADS, HEAD_DIM), 3, axis=2)
        mq = apply_rope(rms_norm(mq[:, :, 0], moba_q_norm[i]), cos, sin)
        mk = apply_rope(rms_norm(mk[:, :, 0], moba_k_norm[i]), cos, sin)
        out_a = moba_attention(mq, mk, mv[:, :, 0])

        out_b = mamba2_mixer(ssm_z, ssm_xbc, ssm_dt, ssm_conv_w[i], ssm_conv_b[i], ssm_dt_bias[i],
                             ssm_a_log[i], ssm_d[i], ssm_out_norm[i])

        dq, dk, dv = jnp.split(dil_qkv.reshape(bsz, S, 3, DIL_GROUPS, DIL_HEADS, HEAD_DIM), 3, axis=2)
        dq = apply_rope(rms_norm(dq[:, :, 0], dil_q_norm[i]), cos, sin)
        dk = apply_rope(rms_norm(dk[:, :, 0], dil_k_norm[i]), cos, sin)
        out_c = dilated_attention(dq, dk, dv[:, :, 0])

        gates = jax.nn.sigmoid(gate_logits + b_gate[i]).reshape(bsz, S, N_BRANCH, D_MODEL)
        merged = (gates[:, :, 0] * (out_a @ w_br_moba[i])
                  + gates[:, :, 1] * (out_b @ w_br_ssm[i])
                  + gates[:, :, 2] * (out_c @ w_br_dil[i]))
        x = x + merged @ w_out[i]

        up = causal_dwconv(rms_norm(x, norm_ffn[i]) @ w_up[i], ffn_conv_w[i], ffn_conv_b[i])
        ga, gb = jnp.split(up, 2, axis=-1)
        x = x + (jax.nn.silu(ga) * gb) @ w_down[i]

        pg = jax.nn.sigmoid(rms_norm(x, norm_ple[i]) @ w_ple_gate[i])
        x = x + (p[i] @ w_ple[i]) * pg
    return x
```

```python
import math
import contextlib
import numpy as np
import ml_dtypes
import concourse.bass as bass
import concourse.mybir as mybir
from concourse.bass_utils import run_bass_kernel_spmd

F32 = mybir.dt.float32
BF16 = mybir.dt.bfloat16
I32 = mybir.dt.int32
AF = mybir.ActivationFunctionType
ALU = mybir.AluOpType
AX = mybir.AxisListType

D = 1024
HD = 64
NEG = -1.0e5
EPS = 1e-6
IN_COLS = 12304
C_MQ, C_MK, C_MV = 0, 512, 1024
C_DQ, C_DK, C_DV = 1536, 3072, 4608
C_Z, C_XBC, C_DT, C_GATE = 6144, 7168, 9216, 9232
FFN = 2816
DIL_RATES = (1, 4, 16)


class R:
    __slots__ = ("name", "lw", "rd")

    def __init__(self, name=""):
        self.name = name
        self.lw = None
        self.rd = {}


class FreshR:
    def __getitem__(self, k):
        return R(k)


class Ctx:
    NDMASEM = 8

    def __init__(self, nc):
        self.nc = nc
        self.eng = {"pe": nc.tensor, "act": nc.scalar, "dve": nc.vector, "pool": nc.gpsimd,
                    "sp": nc.sync}
        self.sems = {}
        self.cnt = {}
        self.seen = {k: {} for k in self.eng}
        self._cms = []
        for k in ("pe", "act", "dve", "pool"):
            self._mksem(k)
            self.cnt[k] = 0
        self.dmacnt = {"sp": 0, "pool": 0}
        for q in ("sp", "pool"):
            for j in range(self.NDMASEM):
                self._mksem(f"dma_{q}_{j}")
        self.n_instr = 0
        self.n_wait = 0

    def _mksem(self, key):
        cm = self.nc.semaphore(key)
        h = cm.__enter__()
        self._cms.append(cm)
        self.sems[key] = h

    def close(self):
        for cm in reversed(self._cms):
            cm.__exit__(None, None, None)

    @staticmethod
    def _need(deps, sk, val):
        if val > deps.get(sk, 0):
            deps[sk] = val

    def _emit_waits(self, ek, deps):
        e = self.eng[ek]
        seen = self.seen[ek]
        for sk, val in deps.items():
            if seen.get(sk, 0) >= val:
                continue
            e.wait_ge(self.sems[sk], val)
            seen[sk] = val
            self.n_wait += 1

    def _deps(self, ek, reads, writes, is_dma):
        deps = {}
        for r in reads:
            if r.lw is not None:
                self._need(deps, *r.lw)
        for w in writes:
            if w.lw is not None and (is_dma or w.lw[0] != ek):
                self._need(deps, *w.lw)
            for sk, val in w.rd.items():
                if is_dma or sk != ek:
                    self._need(deps, sk, val)
        return deps

    def _record(self, tok, reads, writes):
        sk, val = tok
        for r in reads:
            if val > r.rd.get(sk, 0):
                r.rd[sk] = val
        for w in writes:
            w.lw = tok
            w.rd = {}

    def op(self, ek, fn, reads=(), writes=()):
        deps = self._deps(ek, reads, writes, False)
        self._emit_waits(ek, deps)
        ins = fn()
        self.cnt[ek] += 1
        ins.then_inc(self.sems[ek], 1)
        self.n_instr += 1
        self._record((ek, self.cnt[ek]), reads, writes)
        return ins

    def dma(self, q, out, in_, reads=(), writes=(), **kw):
        i = self.dmacnt[q]
        j = i % self.NDMASEM
        sk = f"dma_{q}_{j}"
        deps = self._deps(q, reads, writes, True)
        if i >= self.NDMASEM:
            self._need(deps, sk, 16 * (i // self.NDMASEM))
        self._emit_waits(q, deps)
        ins = self.eng[q].dma_start(out=out, in_=in_, **kw)
        val = 16 * (i // self.NDMASEM + 1)
        ins.then_inc(self.sems[sk], 16)
        self.dmacnt[q] += 1
        self.n_instr += 1
        self._record((sk, val), reads, writes)
        return ins

    def _all_tokens(self):
        deps = {}
        for k in ("pe", "act", "dve", "pool"):
            if self.cnt[k] > 0:
                deps[k] = self.cnt[k]
        K = self.NDMASEM
        for q in ("sp", "pool"):
            n = self.dmacnt[q]
            for j in range(K):
                c = (n - j + K - 1) // K if n > j else 0
                if c > 0:
                    deps[f"dma_{q}_{j}"] = 16 * c
        return deps

    def barrier(self):
        deps = self._all_tokens()
        for ek in ("pe", "act", "dve", "pool", "sp"):
            self._emit_waits(ek, dict(deps))

    def finish(self):
        self._emit_waits("sp", self._all_tokens())


class Prog:
    def __init__(self, T, depth, debug=False):
        self.T = T
        self.NT = T // 128
        self.depth = depth
        self.debug = debug
        self.nc = bass.Bass("TRN2", target_bir_lowering=False)
        self.cx = Ctx(self.nc)
        self.es = contextlib.ExitStack()
        self.dram_in = {}
        self.uid = 0

    def din(self, name, shape, dt=F32):
        t = self.nc.dram_tensor(name, list(shape), dt, kind="ExternalInput").ap()
        self.dram_in[name] = t
        return t

    def dscr(self, name, shape, dt, out=False):
        kind = "ExternalOutput" if (out or self.debug) else "Internal"
        return self.nc.dram_tensor(name, list(shape), dt, kind=kind).ap()

    def sb(self, stack, shape, dt, name=None):
        self.uid += 1
        name = f"{name or 't'}_{self.uid}"
        t = stack.enter_context(self.nc.sbuf_tensor(name, list(shape), dt))
        return t, R(name)

    def sbn(self, stack, n, shape, dt, name=None):
        return [self.sb(stack, shape, dt, name) for _ in range(n)]

    def act(self, out, in_, func, reads, writes, **kw):
        return self.cx.op("act", lambda: self.nc.scalar.activation(out=out, in_=in_, func=func, **kw),
                          reads, writes)

    def ts(self, ek, out, in0, s1, s2, op0, op1, reads, writes):
        e = self.cx.eng[ek]
        if op1 is None:
            return self.cx.op(ek, lambda: e.tensor_scalar(out=out, in0=in0, scalar1=s1, scalar2=None, op0=op0),
                              reads, writes)
        return self.cx.op(ek, lambda: e.tensor_scalar(out=out, in0=in0, scalar1=s1, scalar2=s2, op0=op0, op1=op1),
                          reads, writes)

    def tt(self, ek, out, in0, in1, op, reads, writes):
        e = self.cx.eng[ek]
        return self.cx.op(ek, lambda: e.tensor_tensor(out=out, in0=in0, in1=in1, op=op), reads, writes)

    def cp(self, ek, out, in_, reads, writes):
        e = self.cx.eng[ek]
        if ek == "act":
            return self.cx.op(ek, lambda: e.copy(out=out, in_=in_), reads, writes)
        return self.cx.op(ek, lambda: e.tensor_copy(out=out, in_=in_), reads, writes)

    def mm(self, out, lhsT, rhs, start, stop, reads, writes):
        return self.cx.op("pe", lambda: self.nc.tensor.matmul(out, lhsT, rhs, start=start, stop=stop),
                          reads, writes)

    def tr(self, out, in_, ident, reads, writes):
        return self.cx.op("pe", lambda: self.nc.tensor.transpose(out, in_, ident), reads, writes)


def host_consts(T=4096):
    c = {}
    c["ident_bf"] = np.eye(128, dtype=np.float32).astype(ml_dtypes.bfloat16)
    c["ident_f"] = np.eye(128, dtype=np.float32)
    inv = 500000.0 ** (-np.arange(0, 16, 2, dtype=np.float32) / 16.0)
    c["invf"] = np.tile(inv.astype(np.float32)[None, :], (128, 1))
    k = np.arange(128)[:, None]
    q = np.arange(512)[None, :]
    tb = np.stack([np.where(j * 128 + k <= q, 0.0, NEG) for j in range(4)], 0)
    c["tb_moba"] = tb.transpose(1, 0, 2).astype(ml_dtypes.bfloat16).copy()
    q1 = np.arange(128)[None, :]
    cur = np.where(k <= q1, 0.0, NEG)
    prev = np.where(k >= q1, 0.0, NEG)
    c["tb_dil"] = np.concatenate([cur, prev], 1).astype(ml_dtypes.bfloat16)
    c["tri_f"] = (k <= q1).astype(np.float32)
    NT = T // 128
    n = np.arange(16)[None, :]
    jb = (np.arange(NT) // 2)[:, None]
    c["pm"] = np.tile(np.where(n < jb, 0.0, -1e30).astype(np.float32)[None], (128, 1, 1))
    c["own"] = np.tile((n == jb).astype(np.float32)[None], (128, 1, 1))
    c["ind"] = (np.arange(16)[:, None] == (np.arange(T) // 256)[None, :]).astype(np.float32).astype(ml_dtypes.bfloat16)
    oh = np.zeros((16, 16, 128), np.float32)
    for h in range(16):
        oh[h, h, :] = 1.0
    c["oneh"] = oh.reshape(16, 2048)
    c["ones_f"] = np.ones((128, 128), np.float32)
    return c


def setup(pg):
    nc, cx, T, NT, L = pg.nc, pg.cx, pg.T, pg.NT, pg.depth
    P = pg
    P.x_in = P.din("x", [T, D])
    P.p_in = P.din("p", [L, T, 256])
    P.pos_in = P.din("positions", [T], I32)
    P.w = {}
    for name, shape in [
        ("norm_mix", [L, D]), ("w_in", [L, D, IN_COLS]), ("b_gate", [L, 3 * D]),
        ("moba_q_norm", [L, HD]), ("moba_k_norm", [L, HD]), ("dil_q_norm", [L, HD]), ("dil_k_norm", [L, HD]),
        ("ssm_conv_w", [L, 4, 2048]), ("ssm_conv_b", [L, 2048]), ("ssm_dt_bias", [L, 16]),
        ("ssm_a_log", [L, 16]), ("ssm_d", [L, 16]), ("ssm_out_norm", [L, D]),
        ("w_br_moba", [L, 512, D]), ("w_br_ssm", [L, D, D]), ("w_br_dil", [L, 512, D]),
        ("w_out", [L, D, D]), ("norm_ffn", [L, D]), ("w_up", [L, D, 2 * FFN]),
        ("ffn_conv_w", [L, 3, 2 * FFN]), ("ffn_conv_b", [L, 2 * FFN]), ("w_down", [L, FFN, D]),
        ("norm_ple", [L, D]), ("w_ple_gate", [L, D, D]), ("w_ple", [L, 256, D]),
    ]:
        P.w[name] = P.din(name, shape)
    hc = host_consts(T)
    P.c_in = {}
    for k, v in hc.items():
        P.c_in[k] = P.din("c_" + k, v.shape, BF16 if v.dtype == ml_dtypes.bfloat16 else F32)
    P.xres = nc.dram_tensor("y", [T, D], F32, kind="ExternalOutput").ap()
    P.r_xres = [R(f"xres{i}") for i in range(NT)]
    P.s_mq = P.dscr("s_mq", [T, 512], BF16)
    P.s_mk = P.dscr("s_mk", [T, 512], BF16)
    P.s_mv = P.dscr("s_mv", [T, 512], BF16)
    P.s_dq = P.dscr("s_dq", [3, T, 512], BF16)
    P.s_dk = P.dscr("s_dk", [3, T, 512], BF16)
    P.s_dv = P.dscr("s_dv", [3, T, 512], BF16)
    P.s_sz = P.dscr("s_sz", [T, D], F32)
    P.s_gate = P.dscr("s_gate", [T, 3 * D], BF16)
    P.s_xsT = P.dscr("s_xsT", [D, T], F32)
    P.s_bcT = P.dscr("s_bcT", [D, T], BF16)
    P.s_oa = P.dscr("s_oa", [T, 512], BF16)
    P.s_ob = P.dscr("s_ob", [T, D], BF16)
    P.s_oc = P.dscr("s_oc", [T, 512], BF16)
    P.s_nd = P.dscr("s_nd", [3, T, 8 * 65], F32)
    P.r_scr = FreshR()
    g = pg.es
    P.ident_bf, P.r_const = P.sb(g, [128, 128], BF16, "identbf")
    P.ident_f, _ = P.sb(g, [128, 128], F32, "identf")
    P.invf, _ = P.sb(g, [128, 8], F32, "invf")
    P.tri_f, _ = P.sb(g, [128, 128], F32, "trif")
    rc = P.r_const
    for t, k in [(P.ident_bf, "ident_bf"), (P.ident_f, "ident_f"), (P.invf, "invf"), (P.tri_f, "tri_f")]:
        cx.dma("sp", t[:], P.c_in[k], writes=[rc])
    P.dt_all, P.r_dt = P.sb(g, [128, NT, 16], F32, "dtall")
    P.cos, P.r_rope = P.sb(g, [128, NT, 8], F32, "cos")
    P.sin, _ = P.sb(g, [128, NT, 8], F32, "sin")
    P.ps = []
    for i in range(8):
        t = g.enter_context(nc.psum_tensor(f"psb{i}", [128, 512], F32))
        P.ps.append((t, R(f"ps{i}")))
    P.ps_rr = 0
    P.ps_pool = 6


def next_ps(pg):
    t = pg.ps[pg.ps_rr % pg.ps_pool]
    pg.ps_rr += 1
    return t


def rope_tables(pg):
    nc, cx, NT = pg.nc, pg.cx, pg.NT
    PI = math.pi
    with contextlib.ExitStack() as st:
        posi, r0 = pg.sb(st, [128, NT], I32, "posi")
        posf, r1 = pg.sb(st, [128, NT], F32, "posf")
        ang, r2 = pg.sb(st, [128, NT, 8], F32, "ang")
        kf, r3 = pg.sb(st, [128, NT, 8], F32, "kf")
        ki, r4 = pg.sb(st, [128, NT, 8], I32, "ki")
        m, r5 = pg.sb(st, [128, NT, 8], F32, "m")
        a2, r6 = pg.sb(st, [128, NT, 8], F32, "a2")
        with nc.allow_non_contiguous_dma(reason="tiny positions load"):
            cx.dma("sp", posi[:], pg.pos_in.rearrange("(n p) -> p n", p=128), writes=[r0])
        pg.cp("dve", posf[:], posi[:], [r0], [r1])
        pg.tt("dve", ang[:], posf[:, :].unsqueeze(2).to_broadcast([128, NT, 8]),
              pg.invf[:, :].unsqueeze(1).to_broadcast([128, NT, 8]), ALU.mult, [r1, pg.r_const], [r2])
        pg.ts("dve", kf[:], ang[:], 1.0 / (2 * PI), None, ALU.mult, None, [r2], [r3])
        pg.cp("dve", ki[:], kf[:], [r3], [r4])
        pg.cp("dve", kf[:], ki[:], [r4], [r3])
        pg.ts("dve", kf[:], kf[:], -2 * PI, None, ALU.mult, None, [r3], [r3])
        pg.tt("dve", ang[:], ang[:], kf[:], ALU.add, [r2, r3], [r2])

        def wrap(t, rt):
            pg.ts("dve", m[:], t[:], PI, -2 * PI, ALU.is_gt, ALU.mult, [rt], [r5])
            pg.tt("dve", t[:], t[:], m[:], ALU.add, [rt, r5], [rt])
            pg.ts("dve", m[:], t[:], -PI, 2 * PI, ALU.is_lt, ALU.mult, [rt], [r5])
            pg.tt("dve", t[:], t[:], m[:], ALU.add, [rt, r5], [rt])

        wrap(ang, r2)
        pg.ts("dve", a2[:], ang[:], PI / 2, None, ALU.add, None, [r2], [r6])
        wrap(a2, r6)
        pg.act(pg.sin[:], ang[:], AF.Sin, [r2], [pg.r_rope])
        pg.act(pg.cos[:], a2[:], AF.Sin, [r6], [pg.r_rope])
        cx.barrier()


def lockstep(gens):
    gens = list(gens)
    while gens:
        for g in list(gens):
            try:
                next(g)
            except StopIteration:
                gens.remove(g)


def rms_to_featmajor(pg, st, x_src, r_src, gamma_ap, uT, r_uT, nm):
    nc, cx, NT = pg.nc, pg.cx, pg.NT
    gT, r_g = pg.sb(st, [128, 8], F32, nm + "gT")
    with nc.allow_non_contiguous_dma(reason="small gamma load"):
        cx.dma("sp", gT[:], gamma_ap.rearrange("(kc p) -> p kc", p=128), writes=[r_g])
    G = 3
    with contextlib.ExitStack() as st2:
        xts = pg.sbn(st2, 2 * G, [128, D], F32, nm + "xt")
        xns = pg.sbn(st2, 2 * G, [128, D], BF16, nm + "xn")
        sss = pg.sbn(st2, 2 * G, [128, 1], F32, nm + "ss")

        def tile_gen(i):
            xt, r_xt = xts[i % (2 * G)]
            xn, r_xn = xns[i % (2 * G)]
            ss, r_ss = sss[i % (2 * G)]
            pg.act(xn[:], xt[:], AF.Square, [r_xt], [r_xn, r_ss], accum_out=ss[:, 0:1])
            yield
            pg.ts("dve", ss[:], ss[:], 1.0 / D, EPS, ALU.mult, ALU.add, [r_ss], [r_ss])
            yield
            pg.act(ss[:], ss[:], AF.Sqrt, [r_ss], [r_ss])
            yield
            pg.cx.op("dve", lambda: nc.vector.reciprocal(out=ss[:], in_=ss[:]), [r_ss], [r_ss])
            yield
            pg.ts("dve", xn[:], xt[:], ss[:, 0:1], None, ALU.mult, None, [r_xt, r_ss], [r_xn])
            yield
            pst, r_ps = next_ps(pg)
            psb = pst[:, :].bitcast(BF16)
            for kc in range(8):
                pg.tr(psb[:, kc * 128:(kc + 1) * 128], xn[:, kc * 128:(kc + 1) * 128], pg.ident_bf[:, :],
                      [r_xn, pg.r_const], [r_ps])
            yield
            pg.tt("dve", uT[:, :, i * 128:(i + 1) * 128], psb[:, 0:1024].rearrange("p (k t) -> p k t", k=8),
                  gT[:, :].unsqueeze(2).to_broadcast([128, 8, 128]), ALU.mult, [r_ps, r_g], [r_uT[i]])

        def load(i):
            xt, r_xt = xts[i % (2 * G)]
            cx.dma("sp", xt[:], x_src[i * 128:(i + 1) * 128, :], reads=[r_src[i]] if r_src else [], writes=[r_xt])

        for i in range(min(2 * G, NT)):
            load(i)
        for i0 in range(0, NT, G):
            lockstep([tile_gen(i) for i in range(i0, min(NT, i0 + G))])
            for i in range(i0 + 2 * G, min(NT, i0 + 3 * G)):
                load(i)
        cx.barrier()


def phase1(pg, l):
    nc, cx, T, NT = pg.nc, pg.cx, pg.T, pg.NT
    W = pg.w
    x_src = pg.x_in if l == 0 else pg.xres
    r_src = None if l == 0 else pg.r_xres
    RS = pg.r_scr
    with contextlib.ExitStack() as st:
        uT, _ = pg.sb(st, [128, 8, T], BF16, "uT")
        r_uT = [R(f"uT{i}") for i in range(NT)]
        rms_to_featmajor(pg, st, x_src, r_src, W["norm_mix"][l], uT, r_uT, "p1")

        def bc_load(src_ap, n, nm):
            t, r = pg.sb(st, [128, n], F32, nm)
            cx.dma("sp", t[:], src_ap.partition_broadcast(128), writes=[r])
            return t, r
        g_mq, r_gmq = bc_load(W["moba_q_norm"][l], HD, "gmq")
        g_mk, r_gmk = bc_load(W["moba_k_norm"][l], HD, "gmk")
        g_dq, r_gdq = bc_load(W["dil_q_norm"][l], HD, "gdq")
        g_dk, r_gdk = bc_load(W["dil_k_norm"][l], HD, "gdk")
        bgate, r_bg = bc_load(W["b_gate"][l], 3 * D, "bgate")
        dtb, r_dtb = bc_load(W["ssm_dt_bias"][l], 16, "dtb")
        cw, r_cw = pg.sb(st, [128, 16, 4], F32, "cw")
        cb, r_cb = pg.sb(st, [128, 16], F32, "cb")
        with nc.allow_non_contiguous_dma(reason="small conv weight load"):
            for k in range(4):
                cx.dma("sp", cw[:, :, k], W["ssm_conv_w"][l][k].rearrange("(cc p) -> p cc", p=128), writes=[r_cw])
            cx.dma("sp", cb[:], W["ssm_conv_b"][l].rearrange("(cc p) -> p cc", p=128), writes=[r_cb])

        def rope_tabs(gt, r_gt, nm):
            ta, r_t = pg.sb(st, [128, NT, 16], F32, nm + "A")
            tb, _ = pg.sb(st, [128, NT, 16], F32, nm + "B")
            for half in range(2):
                gb = gt[:, half * 8:(half + 1) * 8].unsqueeze(1).to_broadcast([128, NT, 8])
                pg.tt("pool", ta[:, :, half * 8:(half + 1) * 8], pg.cos[:, :, :], gb, ALU.mult, [pg.r_rope, r_gt], [r_t])
                pg.tt("pool", tb[:, :, half * 8:(half + 1) * 8], pg.sin[:, :, :], gb, ALU.mult, [pg.r_rope, r_gt], [r_t])
            return ta, tb, r_t
        rtab = {}
        for nm, (gt, r_gt) in (("mq", (g_mq, r_gmq)), ("mk", (g_mk, r_gmk)), ("dq", (g_dq, r_gdq)), ("dk", (g_dk, r_gdk))):
            rtab[id(gt)] = rope_tabs(gt, r_gt, "rt" + nm)
        wbs = pg.sbn(st, 2, [128, 8, 512], BF16, "wb")
        wctr = [0]

        def load_w(c0, ncols):
            wb, r_wb = wbs[wctr[0] % 2]
            wctr[0] += 1
            cx.dma("pool", wb[:, :, 0:ncols], W["w_in"][l][:, c0:c0 + ncols].rearrange("(kc p) c -> p kc c", p=128),
                   writes=[r_wb])
            return wb, r_wb

        groups = [(C_MQ, "qk", (g_mq, r_gmq, pg.s_mq, RS["mq"])),
                  (C_MK, "qk", (g_mk, r_gmk, pg.s_mk, RS["mk"])),
                  (C_MV, "v", (pg.s_mv, RS["mv"]))]
        for g in range(3):
            groups.append((C_DQ + g * 512, "qk", (g_dq, r_gdq, pg.s_dq[g], RS["dq"])))
            groups.append((C_DK + g * 512, "qk", (g_dk, r_gdk, pg.s_dk[g], RS["dk"])))
            groups.append((C_DV + g * 512, "v", (pg.s_dv[g], RS["dv"])))
        for j in range(2):
            groups.append((C_Z + j * 512, "z", (j,)))
        for j in range(6):
            groups.append((C_GATE + j * 512, "gate", (j,)))

        sqs = pg.sbn(st, 6, [128, 512], F32, "sq")
        ys = pg.sbn(st, 6, [128, 512], F32, "y")
        obs = pg.sbn(st, 8, [128, 512], BF16, "ob")
        ofs = pg.sbn(st, 2, [128, 512], F32, "of")
        smalls = pg.sbn(st, 6, [128, 8], F32, "ssq")
        rts = pg.sbn(st, 6, [128, 2, 8, 16], F32, "ropet")
        it = [0]

        def post_qk(pst, r_ps, i, gt, r_gt, dest, r_dest):
            k = it[0]
            it[0] += 1
            sq, r_sq = sqs[k % 6]
            y, r_y = ys[k % 6]
            ob, r_ob = obs[k % 8]
            ss, r_ss = smalls[k % 6]
            rt, r_rt = rts[k % 6]
            pg.act(sq[:], pst[:, :], AF.Square, [r_ps], [r_sq])
            yield
            pg.cx.op("dve", lambda: nc.vector.tensor_reduce(out=ss[:], in_=sq[:, :].rearrange("p (h d) -> p h d", h=8),
                                                           axis=AX.X, op=ALU.add), [r_sq], [r_ss])
            pg.ts("dve", ss[:], ss[:], 1.0 / HD, EPS, ALU.mult, ALU.add, [r_ss], [r_ss])
            yield
            pg.act(ss[:], ss[:], AF.Sqrt, [r_ss], [r_ss])
            yield
            pg.cx.op("dve", lambda: nc.vector.reciprocal(out=ss[:], in_=ss[:]), [r_ss], [r_ss])
            y3 = y[:, :].rearrange("p (h d) -> p h d", h=8)
            pg.tt("dve", y3, pst[:, :].rearrange("p (h d) -> p h d", h=8),
                  ss[:, :].unsqueeze(2).to_broadcast([128, 8, HD]), ALU.mult, [r_ps, r_ss], [r_y])
            yield
            ob3 = ob[:, :].rearrange("p (h d) -> p h d", h=8)
            pg.tt("dve", ob3[:, :, 16:64], y3[:, :, 16:64], gt[:, 16:64].unsqueeze(1).to_broadcast([128, 8, 48]),
                  ALU.mult, [r_y, r_gt], [r_ob])
            ta, tb, r_t = rtab[id(gt)]
            y16 = y3[:, :, 0:16]
            pg.tt("pool", rt[:, 0], y16, ta[:, i, :].unsqueeze(1).to_broadcast([128, 8, 16]), ALU.mult,
                  [r_y, r_t], [r_rt])
            pg.tt("pool", rt[:, 1], y16, tb[:, i, :].unsqueeze(1).to_broadcast([128, 8, 16]), ALU.mult,
                  [r_y, r_t], [r_rt])
            yield
            pg.tt("pool", ob3[:, :, 0:8], rt[:, 0, :, 0:8], rt[:, 1, :, 8:16], ALU.subtract, [r_rt], [r_ob])
            pg.tt("pool", ob3[:, :, 8:16], rt[:, 0, :, 8:16], rt[:, 1, :, 0:8], ALU.add, [r_rt], [r_ob])
            yield
            cx.dma("sp", dest[i * 128:(i + 1) * 128, :], ob[:], reads=[r_ob], writes=[r_dest])

        for gi, (c0, kind, info) in enumerate(groups):
            wb, r_wb = load_w(c0, 512)
            if kind == "qk":
                GQ = 3
                for i0 in range(0, NT, GQ):
                    gens = []
                    for i in range(i0, min(NT, i0 + GQ)):
                        pst, r_ps = next_ps(pg)
                        for kc in range(8):
                            pg.mm(pst[:, :], uT[:, kc, i * 128:(i + 1) * 128], wb[:, kc, :], kc == 0, kc == 7,
                                  [r_uT[i], r_wb], [r_ps])
                        gens.append(post_qk(pst, r_ps, i, *info))
                    lockstep(gens)
                continue
            for i in range(NT):
                pst, r_ps = next_ps(pg)
                for kc in range(8):
                    pg.mm(pst[:, :], uT[:, kc, i * 128:(i + 1) * 128], wb[:, kc, :], kc == 0, kc == 7,
                          [r_uT[i], r_wb], [r_ps])
                if kind == "qk":
                    post_qk(pst, r_ps, i, *info)
                elif kind == "v":
                    k = it[0]
                    it[0] += 1
                    ob, r_ob = obs[k % 3]
                    pg.cp("act", ob[:], pst[:, :], [r_ps], [r_ob])
                    cx.dma("sp", info[0][i * 128:(i + 1) * 128, :], ob[:], reads=[r_ob], writes=[info[1]])
                elif kind == "z":
                    k = it[0]
                    it[0] += 1
                    of, r_of = ofs[k % 2]
                    pg.act(of[:], pst[:, :], AF.Silu, [r_ps], [r_of])
                    j = info[0]
                    cx.dma("sp", pg.s_sz[i * 128:(i + 1) * 128, j * 512:(j + 1) * 512], of[:], reads=[r_of],
                           writes=[RS["sz"]])
                elif kind == "gate":
                    k = it[0]
                    it[0] += 1
                    of, r_of = ofs[k % 2]
                    ob, r_ob = obs[k % 3]
                    j = info[0]
                    pg.tt("dve", of[:], pst[:, :], bgate[:, j * 512:(j + 1) * 512], ALU.add, [r_ps, r_bg], [r_of])
                    pg.act(ob[:], of[:], AF.Sigmoid, [r_of], [r_ob])
                    cx.dma("sp", pg.s_gate[i * 128:(i + 1) * 128, j * 512:(j + 1) * 512], ob[:], reads=[r_ob],
                           writes=[RS["gate"]])

        wb, r_wb = load_w(C_DT, 16)
        dts = pg.sbn(st, 2, [128, 16], F32, "dtt")
        for i in range(NT):
            pst, r_ps = next_ps(pg)
            for kc in range(8):
                pg.mm(pst[:, 0:16], uT[:, kc, i * 128:(i + 1) * 128], wb[:, kc, 0:16], kc == 0, kc == 7,
                      [r_uT[i], r_wb], [r_ps])
            dt_, r_dt = dts[i % 2]
            pg.tt("dve", dt_[:], pst[:, 0:16], dtb[:, :], ALU.add, [r_ps, r_dtb], [r_dt])
            pg.act(dt_[:], dt_[:], AF.Exp, [r_dt], [r_dt])
            pg.act(pg.dt_all[:, i, :], dt_[:], AF.Ln, [r_dt], [pg.r_dt], bias=1.0)

        raws = pg.sbn(st, 4, [128, 3 + 512], F32, "raw")
        accs = pg.sbn(st, 4, [128, 512], F32, "acc")
        NG = T // 512
        k = 0
        for cg in range(4):
            wb, r_wb = load_w(C_XBC + cg * 512, 512)
            for cc in range(4):
                ch = cg * 4 + cc
                for tg in range(NG):
                    pst, r_ps = next_ps(pg)
                    for kc in range(8):
                        pg.mm(pst[:, :], wb[:, kc, cc * 128:(cc + 1) * 128], uT[:, kc, tg * 512:(tg + 1) * 512],
                              kc == 0, kc == 7, [r_uT[tg * 4 + j] for j in range(4)] + [r_wb], [r_ps])
                    raw, r_raw = raws[k % 4]
                    praw, r_praw = raws[(k - 1) % 4]
                    acc, r_acc = accs[k % 4]
                    if tg == 0:
                        pg.cx.op("pool", lambda: nc.gpsimd.memset(raw[:, 0:3], 0.0), [], [r_raw])
                    else:
                        pg.cp("pool", raw[:, 0:3], praw[:, 512:515], [r_praw], [r_raw])
                    pg.cp("act", raw[:, 3:515], pst[:, :], [r_ps], [r_raw])
                    pg.ts("dve", acc[:], raw[:, 0:512], cw[:, ch, 0:1], cb[:, ch:ch + 1], ALU.mult, ALU.add,
                          [r_raw, r_cw, r_cb], [r_acc])
                    for kk in range(1, 4):
                        pg.cx.op("dve", lambda kk=kk: nc.vector.scalar_tensor_tensor(
                            out=acc[:], in0=raw[:, kk:kk + 512], scalar=cw[:, ch, kk:kk + 1], in1=acc[:],
                            op0=ALU.mult, op1=ALU.add), [r_raw, r_cw, r_acc], [r_acc])
                    if ch < 8:
                        of, r_of = ofs[k % 2]
                        pg.act(of[:], acc[:], AF.Silu, [r_acc], [r_of])
                        cx.dma("sp", pg.s_xsT[ch * 128:(ch + 1) * 128, tg * 512:(tg + 1) * 512], of[:],
                               reads=[r_of], writes=[RS["xsT"]])
                    else:
                        ob, r_ob = obs[k % 3]
                        pg.act(ob[:], acc[:], AF.Silu, [r_acc], [r_ob])
                        cx.dma("sp", pg.s_bcT[(ch - 8) * 128:(ch - 7) * 128, tg * 512:(tg + 1) * 512], ob[:],
                               reads=[r_ob], writes=[RS["bcT"]])
                    k += 1
        cx.barrier()


def phase2(pg, l):
    nc, cx, T, NT = pg.nc, pg.cx, pg.T, pg.NT
    NS = T // 512
    with contextlib.ExitStack() as st:
        oa, _ = pg.sb(st, [128, NT, 512], BF16, "oa")
        r_oa = [R() for _ in range(NT)]
        pg.pm, r_c2 = pg.sb(st, [128, NT, 16], F32, "pm")
        pg.own, _ = pg.sb(st, [128, NT, 16], F32, "own")
        pg.tb_moba, _ = pg.sb(st, [128, 4, 512], BF16, "tbmoba")
        cx.dma("sp", pg.pm[:], pg.c_in["pm"], writes=[r_c2])
        cx.dma("sp", pg.own[:], pg.c_in["own"], writes=[r_c2])
        cx.dma("sp", pg.tb_moba[:], pg.c_in["tb_moba"], writes=[r_c2])
        kTx, r_kT = pg.sb(st, [80, T], BF16, "kTx")
        qTx, r_qT = pg.sb(st, [80, T], BF16, "qTx")
        cx.dma("sp", kTx[64:80, :], pg.c_in["ind"], writes=[r_kT])
        ktm, r_ktm = pg.sb(st, [128, NT, 64], BF16, "ktm")
        qtm, r_qtm = pg.sb(st, [128, NT, 64], BF16, "qtm")
        vext, r_v = pg.sb(st, [128, NT, 65], BF16, "vext")
        cx.op("pool", lambda: nc.gpsimd.memset(vext[:, :, 64:65], 1.0), [], [r_v])
        kmf, r_kmf = pg.sb(st, [64, 16], F32, "kmf")
        kmb, r_kmb = pg.sb(st, [64, 16], BF16, "kmb")
        cx.op("pool", lambda: nc.gpsimd.memset(kmf[:], 0.0), [], [r_kmf])
        scm, r_scm = pg.sb(st, [128, NT, 16], F32, "scm")
        m8, r_m8 = pg.sb(st, [128, NT, 8], F32, "m8")
        thr, r_thr = pg.sb(st, [128, NT, 1], F32, "thr")
        sel, r_sel = pg.sb(st, [128, NT, 16], F32, "sel")
        btm, r_btm = pg.sb(st, [128, NT, 16], BF16, "btm")
        es = pg.sbn(st, 4, [128, 512], BF16, "e")
        ectr = [0]
        recs = pg.sbn(st, 2, [128, 4], F32, "rec")
        NB = T // 256
        with nc.allow_non_contiguous_dma(reason="per-head 128B rows"):
            for h in range(8):
                hs = slice(h * 64, (h + 1) * 64)
                cx.dma("sp", ktm[:], pg.s_mk[:, hs].rearrange("(n p) d -> p n d", p=128), writes=[r_ktm])
                cx.dma("sp", qtm[:], pg.s_mq[:, hs].rearrange("(n p) d -> p n d", p=128), writes=[r_qtm])
                cx.dma("sp", vext[:, :, 0:64], pg.s_mv[:, hs].rearrange("(n p) d -> p n d", p=128), writes=[r_v])
                for (src, r_src, dst, r_dst) in ((ktm, r_ktm, kTx, r_kT), (qtm, r_qtm, qTx, r_qT)):
                    for i0 in range(0, NT, 8):
                        n = min(8, NT - i0)
                        pst, r_ps = next_ps(pg)
                        psb = pst[:, :].bitcast(BF16)
                        for i in range(n):
                            pg.tr(psb[0:64, i * 128:(i + 1) * 128], src[:, i0 + i, :], pg.ident_bf[:, :],
                                  [r_src, pg.r_const], [r_ps])
                        pg.cp("dve", dst[0:64, i0 * 128:(i0 + n) * 128], psb[0:64, 0:n * 128], [r_ps], [r_dst])
                cx.op("dve", lambda: nc.vector.tensor_reduce(
                    out=kmf[:, 0:NB], in_=kTx[0:64, :].rearrange("p (n k) -> p n k", k=256), axis=AX.X, op=ALU.add),
                    [r_kT], [r_kmf])
                pg.ts("dve", kmb[:], kmf[:], 1.0 / 256, None, ALU.mult, None, [r_kmf], [r_kmb])
                for i in range(NT):
                    pst, r_ps = next_ps(pg)
                    pg.mm(pst[:, 0:16], qTx[0:64, i * 128:(i + 1) * 128], kmb[:, :], True, True, [r_qT, r_kmb], [r_ps])
                    pg.tt("dve", scm[:, i, :], pst[:, 0:16], pg.pm[:, i, :], ALU.add, [r_ps, r_c2], [r_scm])
                    cx.op("dve", lambda i=i: nc.vector.max(out=m8[:, i, :], in_=scm[:, i, :]), [r_scm], [r_m8])
                pg.ts("dve", thr[:], m8[:, :, 2:3], -1e29, None, ALU.max, None, [r_m8], [r_thr])
                pg.tt("dve", sel[:], scm[:], thr[:, :, :].to_broadcast([128, NT, 16]), ALU.is_ge, [r_scm, r_thr], [r_sel])
                pg.tt("dve", sel[:], sel[:], pg.own[:], ALU.add, [r_sel, r_c2], [r_sel])
                pg.ts("dve", btm[:], sel[:], -1.0, -NEG, ALU.add, ALU.mult, [r_sel], [r_btm])
                for i0 in range(0, NT, 8):
                    n = min(8, NT - i0)
                    pst, r_ps = next_ps(pg)
                    psb = pst[:, :].bitcast(BF16)
                    for i in range(n):
                        pg.tr(psb[64:80, i * 128:(i + 1) * 128], btm[:, i0 + i, :], pg.ident_bf[:, :],
                              [r_btm, pg.r_const], [r_ps])
                    pg.cp("dve", qTx[64:80, i0 * 128:(i0 + n) * 128], psb[64:80, 0:n * 128], [r_ps], [r_qT])
                for m in range(NS):
                    po, r_po = pg.ps[6 + (m % 2)]
                    nj = 4 * m + 4

                    def scores(j):
                        pst, r_ps = next_ps(pg)
                        diag = j >= 4 * m
                        pg.mm(pst[:, :], kTx[0:80, j * 128:(j + 1) * 128], qTx[0:80, m * 512:(m + 1) * 512],
                              True, not diag, [r_kT, r_qT], [r_ps])
                        if diag:
                            pg.mm(pst[:, :], pg.ident_bf[:, :], pg.tb_moba[:, j - 4 * m, :], False, True,
                                  [pg.r_const, r_c2], [r_ps])
                        e, r_e = es[ectr[0] % 4]
                        ectr[0] += 1
                        pg.act(e[:], pst[:, :], AF.Exp, [r_ps], [r_e], scale=0.125)
                        return e, r_e

                    def pv(j, e, r_e):
                        for qi in range(4):
                            if j > 4 * m + qi:
                                continue
                            pg.mm(po[:, qi * 65:(qi + 1) * 65], e[:, qi * 128:(qi + 1) * 128], vext[:, j, :],
                                  (j == 0 and qi == 0), (j == 4 * m + qi), [r_e, r_v], [r_po])

                    pend = [(0,) + scores(0)]
                    for j in range(1, nj):
                        pend.append((j,) + scores(j))
                        if len(pend) > 2:
                            pv(*pend.pop(0))
                    while pend:
                        pv(*pend.pop(0))
                    rec, r_rec = recs[m % 2]
                    po3 = po[:, 0:260].rearrange("p (q d) -> p q d", d=65)
                    cx.op("dve", lambda: nc.vector.reciprocal(out=rec[:], in_=po3[:, :, 64]), [r_po], [r_rec])
                    for qi in range(4):
                        pg.ts("dve", oa[:, 4 * m + qi, hs], po[:, qi * 65:qi * 65 + 64], rec[:, qi:qi + 1], None,
                              ALU.mult, None, [r_po, r_rec], [r_oa[4 * m + qi]])
        cx.dma("sp", pg.s_oa.rearrange("(n p) d -> p n d", p=128), oa[:], reads=r_oa)
        cx.barrier()


def phase3(pg, l):
    nc, cx, T, NT = pg.nc, pg.cx, pg.T, pg.NT
    with contextlib.ExitStack() as st:
        qds = pg.sbn(st, 4, [128, 512], BF16, "qd")
        kds = pg.sbn(st, 4, [128, 512], BF16, "kd")
        vxs = pg.sbn(st, 6, [128, 8, 65], BF16, "vx")
        for vx, r_vx in vxs:
            cx.op("pool", lambda vx=vx: nc.gpsimd.memset(vx[:, :, 64:65], 1.0), [], [r_vx])
        qks = pg.sbn(st, 4, [64, 16, 128], BF16, "qkT")
        es = pg.sbn(st, 8, [128, 512], BF16, "e3")
        nds = pg.sbn(st, 2, [128, 8, 65], F32, "nd")
        m01, r_m01 = pg.sb(st, [128, 2, 256], BF16, "m01")
        tbd, r_tbd = pg.sb(st, [128, 256], BF16, "tbdil")
        cx.dma("sp", tbd[:], pg.c_in["tb_dil"], writes=[r_tbd])
        for j in range(2):
            pg.ts("dve", m01[:, j, :], tbd[:, :], -1.0, None, ALU.is_ge, None, [r_tbd], [r_m01])
        m01flat = m01[:, :, :].rearrange("p a t -> p (a t)")
        ectr = [0]

        def loads(k, g, r, c, ti):
            qv = pg.s_dq[g].rearrange("(i r) d -> r i d", r=r)
            kv = pg.s_dk[g].rearrange("(i r) d -> r i d", r=r)
            vv = pg.s_dv[g].rearrange("(i r) d -> r i d", r=r)
            rows = slice(ti * 128, (ti + 1) * 128)
            qd, r_qd = qds[k % 4]
            kd, r_kd = kds[k % 4]
            vx, r_vx = vxs[k % 6]
            cx.dma("sp", qd[:], qv[c, rows, :], writes=[r_qd])
            cx.dma("sp", kd[:], kv[c, rows, :], writes=[r_kd])
            cx.dma("sp", vx[:, :, 0:64], vv[c, rows, :].rearrange("p (h d) -> p h d", h=8), writes=[r_vx])

        def stage_a(k, g, r, c, ti):
            qd, r_qd = qds[k % 4]
            kd, r_kd = kds[k % 4]
            qk, r_qk = qks[k % 4]
            pqk, r_pqk = qks[(k - 1) % 4]
            for which, (src, r_src) in enumerate(((qd, r_qd), (kd, r_kd))):
                pst, r_ps = next_ps(pg)
                psb = pst[:, :].bitcast(BF16)
                for h in range(8):
                    pg.tr(psb[0:64, h * 128:(h + 1) * 128], src[:, h * 64:(h + 1) * 64], pg.ident_bf[:, :],
                          [r_src, pg.r_const], [r_ps])
                pg.cp("dve", qk[:, which * 8:(which + 1) * 8, :],
                      psb[0:64, 0:1024].rearrange("p (a t) -> p a t", a=8), [r_ps], [r_qk])
            has_prev = ti > 0
            elist = []
            for hp in range(4):
                pst, r_ps = next_ps(pg)
                first = True
                for hh in range(2):
                    h = hp * 2 + hh
                    pg.mm(pst[:, hh * 256:hh * 256 + 128], qk[:, 8 + h, :], qk[:, h, :], first, not has_prev,
                          [r_qk], [r_ps])
                    first = False
                    if has_prev:
                        pg.mm(pst[:, hh * 256 + 128:hh * 256 + 256], pqk[:, 8 + h, :], qk[:, h, :], False, True,
                              [r_qk, r_pqk], [r_ps])
                e, r_e = es[ectr[0] % 8]
                ectr[0] += 1
                if has_prev:
                    pg.act(e[:], pst[:, :], AF.Exp, [r_ps], [r_e], scale=0.125)
                    pg.tt("pool", e[:], e[:], m01flat, ALU.mult, [r_e, r_m01], [r_e])
                else:
                    for hh in range(2):
                        cs = slice(hh * 256, hh * 256 + 128)
                        pg.act(e[:, cs], pst[:, cs], AF.Exp, [r_ps], [r_e], scale=0.125)
                        pg.tt("pool", e[:, cs], e[:, cs], m01[:, 0, 0:128], ALU.mult, [r_e, r_m01], [r_e])
                elist.append((e, r_e))
            return (k, g, r, c, ti, elist)

        def stage_b(k, g, r, c, ti, elist):
            ov = pg.s_nd[g].rearrange("(i r) d -> r i d", r=r)
            rows = slice(ti * 128, (ti + 1) * 128)
            vx, r_vx = vxs[k % 6]
            pvx, r_pvx = vxs[(k - 1) % 6]
            nd, r_nd = nds[k % 2]
            has_prev = ti > 0
            pos = [pg.ps[6], pg.ps[7]]
            for hp in range(4):
                e, r_e = elist[hp]
                for hh in range(2):
                    h = hp * 2 + hh
                    po, r_po = pos[h // 4]
                    oc = slice((h % 4) * 65, (h % 4) * 65 + 65)
                    pg.mm(po[:, oc], e[:, hh * 256:hh * 256 + 128], vx[:, h, :], (h % 4 == 0), not has_prev,
                          [r_e, r_vx], [r_po])
                    if has_prev:
                        pg.mm(po[:, oc], e[:, hh * 256 + 128:hh * 256 + 256], pvx[:, h, :], False, True,
                              [r_e, r_pvx], [r_po])
            for half in range(2):
                po, r_po = pos[half]
                pg.cp("act" if half else "dve", nd[:, half * 4:(half + 1) * 4, :],
                      po[:, 0:260].rearrange("p (h d) -> p h d", d=65), [r_po], [r_nd])
            cx.dma("sp", ov[c, rows, :], nd[:, :, :].rearrange("p h d -> p (h d)"), reads=[r_nd])

        tiles = []
        for g, r in enumerate(DIL_RATES):
            nts = (T // r) // 128
            for c in range(r):
                for ti in range(nts):
                    tiles.append((g, r, c, ti))
        pend = None
        for k in range(min(3, len(tiles))):
            loads(k, *tiles[k])
        for k, (g, r, c, ti) in enumerate(tiles):
            if k + 3 < len(tiles):
                loads(k + 3, *tiles[k + 3])
            cur = stage_a(k, g, r, c, ti)
            if pend is not None:
                stage_b(*pend)
            pend = cur
        stage_b(*pend)
        cx.barrier()
        ins = [pg.sbn(st, 4, [128, 8, 65], F32, f"ndin{g}") for g in range(3)]
        recs = pg.sbn(st, 2, [128, 8], F32, "rec3")
        ocs = pg.sbn(st, 2, [128, 8, 64], BF16, "oc")

        def cload(i):
            rows = slice(i * 128, (i + 1) * 128)
            for g in range(3):
                tt_, rr_ = ins[g][i % 4]
                cx.dma("sp", tt_[:, :, :].rearrange("p h d -> p (h d)"), pg.s_nd[g][rows, :], writes=[rr_])
        for i in range(min(3, NT)):
            cload(i)
        for i in range(NT):
            rows = slice(i * 128, (i + 1) * 128)
            if i + 3 < NT:
                cload(i + 3)
            t = [ins[g][i % 4] for g in range(3)]
            pg.tt("dve", t[0][0][:], t[0][0][:], t[1][0][:], ALU.add, [t[0][1], t[1][1]], [t[0][1]])
            pg.tt("dve", t[0][0][:], t[0][0][:], t[2][0][:], ALU.add, [t[0][1], t[2][1]], [t[0][1]])
            rec, r_rec = recs[i % 2]
            oc, r_oc = ocs[i % 2]
            cx.op("dve", lambda: nc.vector.reciprocal(out=rec[:], in_=t[0][0][:, :, 64]), [t[0][1]], [r_rec])
            pg.tt("pool", oc[:], t[0][0][:, :, 0:64], rec[:, :].unsqueeze(2).to_broadcast([128, 8, 64]), ALU.mult,
                  [t[0][1], r_rec], [r_oc])
            cx.dma("sp", pg.s_oc[rows, :], oc[:, :, :].rearrange("p h d -> p (h d)"), reads=[r_oc])
        cx.barrier()


def phase4(pg, l):
    nc, cx, T, NT = pg.nc, pg.cx, pg.T, pg.NT
    W = pg.w
    pg.ps_pool = 5
    with contextlib.ExitStack() as st:
        def bc_load(src_ap, n, nm):
            t, r = pg.sb(st, [128, n], F32, nm)
            cx.dma("sp", t[:], src_ap.partition_broadcast(128), writes=[r])
            return t, r
        pg.oneh, r_c4 = pg.sb(st, [16, 16, 128], F32, "oneh")
        pg.ones_f, _ = pg.sb(st, [128, 128], F32, "onesf")
        cx.dma("sp", pg.oneh[:, :, :].rearrange("p a b -> p (a b)"), pg.c_in["oneh"], writes=[r_c4])
        cx.dma("sp", pg.ones_f[:], pg.c_in["ones_f"], writes=[r_c4])
        a_bc, r_a = bc_load(W["ssm_a_log"][l], 16, "abc")
        pg.act(a_bc[:], a_bc[:], AF.Exp, [r_a], [r_a])
        pg.ts("dve", a_bc[:], a_bc[:], -1.0, None, ALU.mult, None, [r_a], [r_a])
        d_bc, r_d = bc_load(W["ssm_d"][l], 16, "dbc")
        gn_bc, r_gn = bc_load(W["ssm_out_norm"][l], D, "gnbc")
        state, _ = pg.sb(st, [128, 16, 64], F32, "state")
        state_bf, _ = pg.sb(st, [128, 16, 64], BF16, "statebf")
        r_st = [R(), R()]
        r_stb = [R(), R()]
        for hf in range(2):
            hsl = slice(hf * 8, hf * 8 + 8)
            cx.op("pool", lambda: nc.gpsimd.memset(state[:, hsl, :], 0.0), [], [r_st[hf]])
            cx.op("pool", lambda: nc.gpsimd.memset(state_bf[:, hsl, :], 0.0), [], [r_stb[hf]])
        NBUF = 4
        NL = 6
        xsTs = pg.sbn(st, NL, [128, 8, 128], F32, "xsTt")
        bcTs = pg.sbn(st, NL, [128, 8, 128], BF16, "bcTt")
        szs = pg.sbn(st, NL, [128, D], F32, "szt")

        def loads(i):
            tok = slice(i * 128, (i + 1) * 128)
            cx.dma("sp", xsTs[i % NL][0][:], pg.s_xsT[:, tok].rearrange("(c p) t -> p c t", p=128), writes=[xsTs[i % NL][1]])
            cx.dma("sp", bcTs[i % NL][0][:], pg.s_bcT[:, tok].rearrange("(c p) t -> p c t", p=128), writes=[bcTs[i % NL][1]])
            cx.dma("sp", szs[i % NL][0][:], pg.s_sz[tok, :], writes=[szs[i % NL][1]])
        xtms = pg.sbn(st, NBUF, [128, 16, 64], F32, "xtm")
        btms = pg.sbn(st, NBUF, [128, 512], BF16, "btm4")
        adts = pg.sbn(st, NBUF, [128, 16], F32, "adt")
        acss = pg.sbn(st, NBUF, [128, 16], F32, "acs")
        eacss = pg.sbn(st, NBUF, [128, 16], F32, "eacs")
        dsts = pg.sbn(st, NBUF, [128, 16], F32, "dst")
        cds = pg.sbn(st, NBUF, [128, 16], F32, "cd")
        acsTs = pg.sbn(st, NBUF, [16, 128], F32, "acsT")
        xdts = pg.sbn(st, NBUF, [128, 16, 64], BF16, "xdt")
        xdtds = pg.sbn(st, NBUF, [128, 16, 64], BF16, "xdtd")
        cbms = pg.sbn(st, 4, [128, 128], F32, "cbm")
        dds = pg.sbn(st, 4, [128, 4, 128], F32, "dd")
        mts = pg.sbn(st, NBUF * 4, [128, 4, 128], BF16, "mt")
        ysbs = pg.sbn(st, 2, [128, 16, 64], F32, "ysb")
        r_ys = [[R(), R()], [R(), R()]]
        xds = pg.sbn(st, NBUF, [128, 16, 64], F32, "xd")
        junks = pg.sbn(st, 2, [128, D], BF16, "junk4")
        sss = pg.sbn(st, 2, [128, 1], F32, "ss4")
        obs = pg.sbn(st, 2, [128, D], BF16, "ob4")
        kkc = [0]

        def stage_a(i):
            tok = slice(i * 128, (i + 1) * 128)
            b = i % NBUF
            xsT_t, r_xsT = xsTs[i % NL]
            bcT_t, r_bcT = bcTs[i % NL]
            x_tm, r_xtm = xtms[b]
            B_tm, r_btm = btms[b]
            adt, r_adt = adts[b]
            acs, r_acs = acss[b]
            eacs, r_eacs = eacss[b]
            dst, r_dst = dsts[b]
            cd, r_cd = cds[b]
            acsT, r_acsT = acsTs[b]
            xdt, r_xdt = xdts[b]
            xdtd, r_xdtd = xdtds[b]
            xd, r_xd = xds[b]
            for half in range(2):
                pst, r_ps = next_ps(pg)
                for c in range(4):
                    pg.tr(pst[:, c * 128:(c + 1) * 128], xsT_t[:, half * 4 + c, :], pg.ident_f[:, :],
                          [r_xsT, pg.r_const], [r_ps])
                pg.cp("act", x_tm[:, half * 8:(half + 1) * 8, :], pst[:, :].rearrange("p (h d) -> p h d", d=64),
                      [r_ps], [r_xtm])
            pst, r_ps = next_ps(pg)
            psb = pst[:, :].bitcast(BF16)
            for g in range(4):
                pg.tr(psb[:, g * 128:(g + 1) * 128], bcT_t[:, g, :], pg.ident_bf[:, :], [r_bcT, pg.r_const], [r_ps])
            pg.cp("act", B_tm[:], psb[:, 0:512], [r_ps], [r_btm])
            pg.tt("dve", adt[:], pg.dt_all[:, i, :], a_bc[:], ALU.mult, [pg.r_dt, r_a], [r_adt])
            yield
            ps1, r_ps1 = next_ps(pg)
            pg.mm(ps1[:, 0:16], pg.tri_f[:, :], adt[:], True, False, [r_adt, pg.r_const], [r_ps1])
            pg.mm(ps1[:, 16:32], pg.ones_f[:, :], adt[:], False, True, [r_adt, r_c4], [r_ps1])
            ps2, r_ps2 = next_ps(pg)
            pg.mm(ps2[0:16, 0:128], adt[:], pg.tri_f[:, :], True, True, [r_adt, pg.r_const], [r_ps2])
            yield
            pg.cp("dve", acs[:], ps1[:, 0:16], [r_ps1], [r_acs])
            pg.act(eacs[:], ps1[:, 0:16], AF.Exp, [r_ps1], [r_eacs])
            pg.act(cd[:], ps1[:, 16:32], AF.Exp, [r_ps1], [r_cd])
            pg.tt("dve", dst[:], ps1[:, 16:32], acs[:], ALU.subtract, [r_ps1, r_acs], [r_dst])
            pg.act(dst[:], dst[:], AF.Exp, [r_dst], [r_dst])
            pg.cp("act", acsT[:], ps2[0:16, 0:128], [r_ps2], [r_acsT])
            yield
            pg.tt("dve", xdt[:], x_tm[:], pg.dt_all[:, i, :].unsqueeze(2).to_broadcast([128, 16, 64]), ALU.mult,
                  [r_xtm, pg.r_dt], [r_xdt])
            pg.tt("pool", xdtd[:], xdt[:], dst[:, :].unsqueeze(2).to_broadcast([128, 16, 64]), ALU.mult,
                  [r_xdt, r_dst], [r_xdtd])
            pg.tt("pool", xd[:], x_tm[:], d_bc[:, :].unsqueeze(2).to_broadcast([128, 16, 64]), ALU.mult,
                  [r_xtm, r_d], [r_xd])
            for g in range(4):
                cbm, r_cbm = cbms[kkc[0] % 4]
                dd, r_dd = dds[kkc[0] % 4]
                kkc[0] += 1
                mt, r_mt = mts[b * 4 + g]
                yield
                pcb, r_pcb = next_ps(pg)
                pg.mm(pcb[:, 0:128], bcT_t[:, g, :], bcT_t[:, 4 + g, :], True, True, [r_bcT], [r_pcb])
                pd, r_pd = next_ps(pg)
                for j in range(4):
                    h = 4 * g + j
                    pg.mm(pd[:, j * 128:(j + 1) * 128], pg.oneh[:, h, :], acsT[:, :], j == 0, j == 3,
                          [r_acsT, r_c4], [r_pd])
                yield
                pg.tt("dve", cbm[:], pcb[:, 0:128], pg.tri_f[:, :], ALU.mult, [r_pcb, pg.r_const], [r_cbm])
                for j in range(4):
                    h = 4 * g + j
                    pg.ts("dve", dd[:, j, :], pd[:, j * 128:(j + 1) * 128], acs[:, h:h + 1], 0.0,
                          ALU.subtract, ALU.min, [r_pd, r_acs], [r_dd])
                yield
                pg.act(dd[:], dd[:], AF.Exp, [r_dd], [r_dd])
                yield
                pg.tt("pool", mt[:], dd[:], cbm[:, :].unsqueeze(1).to_broadcast([128, 4, 128]), ALU.mult,
                      [r_dd, r_cbm], [r_mt])

        def stage_b(i):
            tok = slice(i * 128, (i + 1) * 128)
            b = i % NBUF
            bcT_t, r_bcT = bcTs[i % NL]
            sz_t, r_sz = szs[i % NL]
            B_tm, r_btm = btms[b]
            eacs, r_eacs = eacss[b]
            cd, r_cd = cds[b]
            xdt, r_xdt = xdts[b]
            xdtd, r_xdtd = xdtds[b]
            xd, r_xd = xds[b]
            y_sb, _ = ysbs[i % 2]
            r_y = r_ys[i % 2]
            for hf in range(2):
                p_y, r_py = pg.ps[5]
                p_off, r_poff = pg.ps[6]
                p_st, r_pst = pg.ps[7]
                hsl = slice(hf * 8, hf * 8 + 8)
                for gg in range(2):
                    g = hf * 2 + gg
                    mt, r_mt = mts[b * 4 + g]
                    for j in range(4):
                        h = 4 * g + j
                        hl = h - hf * 8
                        pg.mm(p_y[:, hl * 64:(hl + 1) * 64], mt[:, j, :], xdt[:, h, :], (gg == 0 and j == 0),
                              (gg == 1 and j == 3), [r_mt, r_xdt], [r_py])
                    pg.mm(p_off[:, gg * 256:(gg + 1) * 256], bcT_t[:, 4 + g, :],
                          state_bf[:, 4 * g:4 * g + 4, :].rearrange("p h d -> p (h d)"), gg == 0, gg == 1,
                          [r_bcT, r_stb[hf]], [r_poff])
                    pg.mm(p_st[:, gg * 256:(gg + 1) * 256], B_tm[:, g * 128:(g + 1) * 128],
                          xdtd[:, 4 * g:4 * g + 4, :].rearrange("p h d -> p (h d)"), gg == 0, gg == 1,
                          [r_btm, r_xdtd], [r_pst])
                pg.tt("dve", y_sb[:, hsl, :], p_off[:, :].rearrange("p (h d) -> p h d", d=64),
                      eacs[:, hsl].unsqueeze(2).to_broadcast([128, 8, 64]), ALU.mult, [r_poff, r_eacs], [r_y[hf]])
                pg.tt("dve", y_sb[:, hsl, :], y_sb[:, hsl, :], p_y[:, :].rearrange("p (h d) -> p h d", d=64), ALU.add,
                      [r_y[hf], r_py], [r_y[hf]])
                pg.tt("pool", state[:, hsl, :], state[:, hsl, :], cd[:, hsl].unsqueeze(2).to_broadcast([128, 8, 64]),
                      ALU.mult, [r_st[hf], r_cd], [r_st[hf]])
                pg.tt("dve", state[:, hsl, :], state[:, hsl, :], p_st[:, :].rearrange("p (h d) -> p h d", d=64), ALU.add,
                      [r_st[hf], r_pst], [r_st[hf]])
                pg.cp("act", state_bf[:, hsl, :], state[:, hsl, :], [r_st[hf]], [r_stb[hf]])
            yf = y_sb[:, :, :].rearrange("p h d -> p (h d)")
            pg.tt("pool", yf, yf, xd[:, :, :].rearrange("p h d -> p (h d)"), ALU.add, [r_y[0], r_y[1], r_xd], [r_y[0], r_y[1]])
            pg.tt("dve", yf, yf, sz_t[:], ALU.mult, [r_y[0], r_y[1], r_sz], [r_y[0], r_y[1]])
            jk, r_jk = junks[i % 2]
            ss, r_ss = sss[i % 2]
            ob, r_ob = obs[i % 2]
            pg.act(jk[:], yf, AF.Square, [r_y[0], r_y[1]], [r_jk, r_ss], accum_out=ss[:, 0:1])
            pg.ts("dve", ss[:], ss[:], 1.0 / D, EPS, ALU.mult, ALU.add, [r_ss], [r_ss])
            pg.act(ss[:], ss[:], AF.Sqrt, [r_ss], [r_ss])
            cx.op("dve", lambda: nc.vector.reciprocal(out=ss[:], in_=ss[:]), [r_ss], [r_ss])
            pg.ts("dve", yf, yf, ss[:, 0:1], None, ALU.mult, None, [r_y[0], r_y[1], r_ss], [r_y[0], r_y[1]])
            pg.tt("pool", ob[:], yf, gn_bc[:], ALU.mult, [r_y[0], r_y[1], r_gn], [r_ob])
            cx.dma("sp", pg.s_ob[tok, :], ob[:], reads=[r_ob])

        NP = NT // 2
        for i in range(min(4, NT)):
            loads(i)
        lockstep([stage_a(0)]); lockstep([stage_a(1)])
        for p in range(NP):
            for i in (2 * p + 4, 2 * p + 5):
                if i < NT:
                    loads(i)
            if p + 1 < NP:
                lockstep([stage_a(2 * p + 2)]); lockstep([stage_a(2 * p + 3)])
            stage_b(2 * p)
            stage_b(2 * p + 1)
        cx.barrier()
    pg.ps_pool = 6


def load_weight(pg, st, w_ap, K, N, name, cchunk=1024):
    kc = K // 128
    t, r = pg.sb(st, [128, kc, N], BF16, name)
    wv = w_ap.rearrange("(kc p) n -> p kc n", p=128)
    for c0 in range(0, N, cchunk):
        c1 = min(N, c0 + cchunk)
        pg.cx.dma("pool", t[:, :, c0:c1], wv[:, :, c0:c1], writes=[r])
    return t, r


def transpose_tile(pg, src, r_src, nblk, dst, r_dst, ek="dve"):
    for b0 in range(0, nblk, 8):
        n = min(8, nblk - b0)
        pst, r_ps = next_ps(pg)
        psb = pst[:, :].bitcast(BF16)
        for k in range(n):
            pg.tr(psb[:, k * 128:(k + 1) * 128], src[:, (b0 + k) * 128:(b0 + k + 1) * 128], pg.ident_bf[:, :],
                  [r_src, pg.r_const], [r_ps])
        pg.cp(ek, dst[:, b0:b0 + n, :], psb[:, 0:n * 128].rearrange("p (k t) -> p k t", k=n), [r_ps], [r_dst])


def phase5(pg, l):
    nc, cx, T, NT = pg.nc, pg.cx, pg.T, pg.NT
    W = pg.w
    x_src = pg.x_in if l == 0 else pg.xres
    with contextlib.ExitStack() as st:
        wa, r_wa = load_weight(pg, st, W["w_br_moba"][l], 512, D, "wa")
        wm, r_wm = load_weight(pg, st, W["w_br_ssm"][l], D, D, "wm")
        wc, r_wc = load_weight(pg, st, W["w_br_dil"][l], 512, D, "wc")
        wo, r_wo = load_weight(pg, st, W["w_out"][l], D, D, "wo")
        oas = pg.sbn(st, 4, [128, 512], BF16, "oat")
        obs_ = pg.sbn(st, 4, [128, D], BF16, "obt")
        ocs = pg.sbn(st, 4, [128, 512], BF16, "oct")
        gts = pg.sbn(st, 4, [128, 3 * D], BF16, "gt")
        xts = pg.sbn(st, 4, [128, D], F32, "xt5")
        oaTs = pg.sbn(st, 2, [128, 4, 128], BF16, "oaT")
        obTs = pg.sbn(st, 2, [128, 8, 128], BF16, "obT")
        ocTs = pg.sbn(st, 2, [128, 4, 128], BF16, "ocT")
        t1s = pg.sbn(st, 2, [128, 512], F32, "t1")
        t2s = pg.sbn(st, 2, [128, 512], F32, "t2")
        mgs = pg.sbn(st, 2, [128, D], BF16, "mg")
        mgTs = pg.sbn(st, 2, [128, 8, 128], BF16, "mgT")
        kkc = [0]

        def loads(i):
            tok = slice(i * 128, (i + 1) * 128)
            b4 = i % 4
            cx.dma("sp", oas[b4][0][:], pg.s_oa[tok, :], writes=[oas[b4][1]])
            cx.dma("sp", obs_[b4][0][:], pg.s_ob[tok, :], writes=[obs_[b4][1]])
            cx.dma("sp", ocs[b4][0][:], pg.s_oc[tok, :], writes=[ocs[b4][1]])
            cx.dma("sp", gts[b4][0][:], pg.s_gate[tok, :], writes=[gts[b4][1]])
            cx.dma("sp", xts[b4][0][:], x_src[tok, :], reads=([pg.r_xres[i]] if l > 0 else []), writes=[xts[b4][1]])

        def stage_a(i):
            tok = slice(i * 128, (i + 1) * 128)
            b = i % 2
            oa, r_oa = oas[i % 4]
            ob, r_ob = obs_[i % 4]
            oc, r_oc = ocs[i % 4]
            gt, r_gt = gts[i % 4]
            oaT, r_oaT = oaTs[b]
            obT, r_obT = obTs[b]
            ocT, r_ocT = ocTs[b]
            mg, r_mg = mgs[b]
            transpose_tile(pg, oa, r_oa, 4, oaT, r_oaT, "dve")
            transpose_tile(pg, ob, r_ob, 8, obT, r_obT, "act")
            transpose_tile(pg, oc, r_oc, 4, ocT, r_ocT, "dve")
            for half in range(2):
                cs = slice(half * 512, (half + 1) * 512)
                t1, r_t1 = t1s[kkc[0] % 2]
                t2, r_t2 = t2s[kkc[0] % 2]
                kkc[0] += 1
                pa, r_pa = next_ps(pg)
                for kc in range(4):
                    pg.mm(pa[:, :], oaT[:, kc, :], wa[:, kc, cs], kc == 0, kc == 3, [r_oaT, r_wa], [r_pa])
                pm, r_pm = next_ps(pg)
                for kc in range(8):
                    pg.mm(pm[:, :], obT[:, kc, :], wm[:, kc, cs], kc == 0, kc == 7, [r_obT, r_wm], [r_pm])
                pc, r_pc = next_ps(pg)
                for kc in range(4):
                    pg.mm(pc[:, :], ocT[:, kc, :], wc[:, kc, cs], kc == 0, kc == 3, [r_ocT, r_wc], [r_pc])
                pg.tt("dve", t1[:], pa[:, :], gt[:, half * 512:(half + 1) * 512], ALU.mult, [r_pa, r_gt], [r_t1])
                pg.tt("dve", t2[:], pm[:, :], gt[:, D + half * 512:D + (half + 1) * 512], ALU.mult, [r_pm, r_gt], [r_t2])
                pg.tt("pool", t1[:], t1[:], t2[:], ALU.add, [r_t1, r_t2], [r_t1])
                pg.tt("dve", t2[:], pc[:, :], gt[:, 2 * D + half * 512:2 * D + (half + 1) * 512], ALU.mult,
                      [r_pc, r_gt], [r_t2])
                pg.tt("pool", mg[:, cs], t1[:], t2[:], ALU.add, [r_t1, r_t2], [r_mg])

        def stage_b(i):
            tok = slice(i * 128, (i + 1) * 128)
            b = i % 2
            xt, r_xt = xts[i % 4]
            mg, r_mg = mgs[b]
            mgT, r_mgT = mgTs[b]
            transpose_tile(pg, mg, r_mg, 8, mgT, r_mgT, "act")
            for half in range(2):
                cs = slice(half * 512, (half + 1) * 512)
                po, r_po = next_ps(pg)
                for kc in range(8):
                    pg.mm(po[:, :], mgT[:, kc, :], wo[:, kc, cs], kc == 0, kc == 7, [r_mgT, r_wo], [r_po])
                pg.tt("dve", xt[:, cs], xt[:, cs], po[:, :], ALU.add, [r_xt, r_po], [r_xt])
            cx.dma("sp", pg.xres[tok, :], xt[:], reads=[r_xt], writes=[pg.r_xres[i]])

        for i in range(min(3, NT)):
            loads(i)
        stage_a(0)
        for i in range(NT):
            if i + 3 < NT:
                loads(i + 3)
            if i + 1 < NT:
                stage_a(i + 1)
            stage_b(i)
        cx.barrier()


def rms_tile(pg, xt, r_xt, xn, r_xn, jk, r_jk, ss, r_ss):
    nc = pg.nc
    pg.act(jk[:], xt[:], AF.Square, [r_xt], [r_jk, r_ss], accum_out=ss[:, 0:1])
    pg.ts("dve", ss[:], ss[:], 1.0 / D, EPS, ALU.mult, ALU.add, [r_ss], [r_ss])
    pg.act(ss[:], ss[:], AF.Sqrt, [r_ss], [r_ss])
    pg.cx.op("dve", lambda: nc.vector.reciprocal(out=ss[:], in_=ss[:]), [r_ss], [r_ss])
    pg.ts("dve", xn[:], xt[:], ss[:, 0:1], None, ALU.mult, None, [r_xt, r_ss], [r_xn])


def phase6(pg, l):
    nc, cx, T, NT = pg.nc, pg.cx, pg.T, pg.NT
    W = pg.w
    NJ = FFN // 128
    NG = T // 512
    with contextlib.ExitStack() as st:
        wup, r_wup = load_weight(pg, st, W["w_up"][l], D, 2 * FFN, "wup", cchunk=1408)
        wdn, r_wdn = load_weight(pg, st, W["w_down"][l], FFN, D, "wdn")
        gT, r_g = pg.sb(st, [128, 8], F32, "gT6")
        cw, r_cw = pg.sb(st, [128, 2 * NJ, 3], F32, "cw6")
        cb, r_cb = pg.sb(st, [128, 2 * NJ], F32, "cb6")
        with nc.allow_non_contiguous_dma(reason="small per-layer vectors"):
            cx.dma("sp", gT[:], W["norm_ffn"][l].rearrange("(kc p) -> p kc", p=128), writes=[r_g])
            for k in range(3):
                cx.dma("sp", cw[:, :, k], W["ffn_conv_w"][l][k].rearrange("(cc p) -> p cc", p=128), writes=[r_cw])
            cx.dma("sp", cb[:], W["ffn_conv_b"][l].rearrange("(cc p) -> p cc", p=128), writes=[r_cb])
        halo, r_halo = pg.sb(st, [128, 2 * NJ, 2], F32, "halo")
        cx.op("pool", lambda: nc.gpsimd.memset(halo[:], 0.0), [], [r_halo])
        uTs = pg.sbn(st, 2, [128, 8, 512], BF16, "uT6")
        hTs = pg.sbn(st, 1, [128, NJ, 512], BF16, "hT")
        xts = pg.sbn(st, 2, [128, D], F32, "xt6")
        xcs = pg.sbn(st, 2, [128, D], F32, "xc6")
        xns = pg.sbn(st, 2, [128, D], BF16, "xn6")
        sss = pg.sbn(st, 2, [128, 1], F32, "ss6")
        raws = pg.sbn(st, 2, [128, 2 + 512], F32, "raw6")
        accs = pg.sbn(st, 3, [128, 512], F32, "acc6")
        kkc = [0]
        k2c = [0]

        def part_a(tg):
            uT, r_uT = uTs[tg % 2]

            def gen(ti):
                i = tg * 4 + ti
                xt, r_xt = xts[ti % 2]
                xn, r_xn = xns[ti % 2]
                ss, r_ss = sss[ti % 2]
                cx.dma("sp", xt[:], pg.xres[i * 128:(i + 1) * 128, :], reads=[pg.r_xres[i]], writes=[r_xt])
                yield
                pg.act(xn[:], xt[:], AF.Square, [r_xt], [r_xn, r_ss], accum_out=ss[:, 0:1])
                yield
                pg.ts("dve", ss[:], ss[:], 1.0 / D, EPS, ALU.mult, ALU.add, [r_ss], [r_ss])
                yield
                pg.act(ss[:], ss[:], AF.Sqrt, [r_ss], [r_ss])
                yield
                cx.op("dve", lambda: nc.vector.reciprocal(out=ss[:], in_=ss[:]), [r_ss], [r_ss])
                pg.ts("dve", xn[:], xt[:], ss[:, 0:1], None, ALU.mult, None, [r_xt, r_ss], [r_xn])
                yield
                pst, r_ps = next_ps(pg)
                psb = pst[:, :].bitcast(BF16)
                for kc in range(8):
                    pg.tr(psb[:, kc * 128:(kc + 1) * 128], xn[:, kc * 128:(kc + 1) * 128], pg.ident_bf[:, :],
                          [r_xn, pg.r_const], [r_ps])
                yield
                pg.tt("dve", uT[:, :, ti * 128:(ti + 1) * 128], psb[:, 0:1024].rearrange("p (k t) -> p k t", k=8),
                      gT[:, :].unsqueeze(2).to_broadcast([128, 8, 128]), ALU.mult, [r_ps, r_g], [r_uT])
            lockstep([gen(0), gen(1)])
            lockstep([gen(2), gen(3)])

        def part_b(tg):
            uT, r_uT = uTs[tg % 2]
            hT, r_hT = hTs[0]
            for j in range(NJ):
                accp = []
                for part in range(2):
                    ch = part * NJ + j
                    pst, r_ps = next_ps(pg)
                    for kc in range(8):
                        pg.mm(pst[:, :], wup[:, kc, ch * 128:(ch + 1) * 128], uT[:, kc, :], kc == 0, kc == 7,
                              [r_wup, r_uT], [r_ps])
                    raw, r_raw = raws[k2c[0] % 2]
                    acc, r_acc = accs[k2c[0] % 3]
                    k2c[0] += 1
                    pg.cp("pool", raw[:, 0:2], halo[:, ch, :], [r_halo], [r_raw])
                    pg.cp("act", raw[:, 2:514], pst[:, :], [r_ps], [r_raw])
                    pg.cp("pool", halo[:, ch, :], raw[:, 512:514], [r_raw], [r_halo])
                    pg.ts("dve", acc[:], raw[:, 0:512], cw[:, ch, 0:1], cb[:, ch:ch + 1], ALU.mult, ALU.add,
                          [r_raw, r_cw, r_cb], [r_acc])
                    for q in range(1, 3):
                        cx.op("dve", lambda q=q: nc.vector.scalar_tensor_tensor(
                            out=acc[:], in0=raw[:, q:q + 512], scalar=cw[:, ch, q:q + 1], in1=acc[:],
                            op0=ALU.mult, op1=ALU.add), [r_raw, r_cw, r_acc], [r_acc])
                    accp.append((acc, r_acc))
                pg.act(accp[0][0][:], accp[0][0][:], AF.Silu, [accp[0][1]], [accp[0][1]])
                pg.tt("pool", hT[:, j, :], accp[0][0][:], accp[1][0][:], ALU.mult, [accp[0][1], accp[1][1]], [r_hT])

        def part_c(tg):
            hT, r_hT = hTs[0]
            def ld(ti):
                i = tg * 4 + ti
                xt, r_xt = xcs[ti % 2]
                cx.dma("sp", xt[:], pg.xres[i * 128:(i + 1) * 128, :], reads=[pg.r_xres[i]], writes=[r_xt])
            ld(0)
            ld(1)
            for ti in range(4):
                i = tg * 4 + ti
                xt, r_xt = xcs[ti % 2]
                if ti >= 1 and ti + 1 < 4:
                    ld(ti + 1)
                for half in range(2):
                    cs = slice(half * 512, (half + 1) * 512)
                    po, r_po = next_ps(pg)
                    for j in range(NJ):
                        pg.mm(po[:, :], hT[:, j, ti * 128:(ti + 1) * 128], wdn[:, j, cs], j == 0, j == NJ - 1,
                              [r_hT, r_wdn], [r_po])
                    pg.tt("dve", xt[:, cs], xt[:, cs], po[:, :], ALU.add, [r_xt, r_po], [r_xt])
                cx.dma("sp", pg.xres[i * 128:(i + 1) * 128, :], xt[:], reads=[r_xt], writes=[pg.r_xres[i]])

        part_a(0)
        for tg in range(NG):
            part_b(tg)
            if tg + 1 < NG:
                part_a(tg + 1)
            part_c(tg)
        cx.barrier()


def phase7(pg, l):
    nc, cx, T, NT = pg.nc, pg.cx, pg.T, pg.NT
    W = pg.w
    with contextlib.ExitStack() as st:
        wpg, r_wpg = load_weight(pg, st, W["w_ple_gate"][l], D, D, "wpg")
        wpl, r_wpl = load_weight(pg, st, W["w_ple"][l], 256, D, "wpl")
        gT, r_g = pg.sb(st, [128, 8], F32, "gT7")
        with nc.allow_non_contiguous_dma(reason="small gamma load"):
            cx.dma("sp", gT[:], W["norm_ple"][l].rearrange("(kc p) -> p kc", p=128), writes=[r_g])
        NX = 6
        xts = pg.sbn(st, NX, [128, D], F32, "xt7")
        xns = pg.sbn(st, 4, [128, D], BF16, "xn7")
        sss = pg.sbn(st, 4, [128, 1], F32, "ss7")
        uTs = pg.sbn(st, 4, [128, 8, 128], BF16, "uT7")
        pfs = pg.sbn(st, NX, [128, 256], F32, "pf")
        pbs = pg.sbn(st, 4, [128, 256], BF16, "pb")
        pTs = pg.sbn(st, 4, [128, 2, 128], BF16, "pT")
        sgs = pg.sbn(st, 2, [128, 512], F32, "sg")
        t1s = pg.sbn(st, 2, [128, 512], F32, "t17")
        kkc = [0]

        def loads(i):
            tok = slice(i * 128, (i + 1) * 128)
            xt, r_xt = xts[i % NX]
            pf, r_pf = pfs[i % NX]
            cx.dma("sp", xt[:], pg.xres[tok, :], reads=[pg.r_xres[i]], writes=[r_xt])
            cx.dma("sp", pf[:], pg.p_in[l][tok, :], writes=[r_pf])

        def stage_a(i):
            b = i % 4
            xt, r_xt = xts[i % NX]
            xn, r_xn = xns[b]
            ss, r_ss = sss[b]
            uT, r_uT = uTs[b]
            pf, r_pf = pfs[i % NX]
            pb, r_pb = pbs[b]
            pT, r_pT = pTs[b]
            pg.act(xn[:], xt[:], AF.Square, [r_xt], [r_xn, r_ss], accum_out=ss[:, 0:1])
            pg.cp("pool", pb[:], pf[:], [r_pf], [r_pb])
            yield
            pg.ts("dve", ss[:], ss[:], 1.0 / D, EPS, ALU.mult, ALU.add, [r_ss], [r_ss])
            yield
            pg.act(ss[:], ss[:], AF.Sqrt, [r_ss], [r_ss])
            yield
            cx.op("dve", lambda: nc.vector.reciprocal(out=ss[:], in_=ss[:]), [r_ss], [r_ss])
            pg.ts("dve", xn[:], xt[:], ss[:, 0:1], None, ALU.mult, None, [r_xt, r_ss], [r_xn])
            yield
            pst, r_ps = next_ps(pg)
            psb = pst[:, :].bitcast(BF16)
            for kc in range(8):
                pg.tr(psb[:, kc * 128:(kc + 1) * 128], xn[:, kc * 128:(kc + 1) * 128], pg.ident_bf[:, :],
                      [r_xn, pg.r_const], [r_ps])
            yield
            pg.tt("dve", uT[:, :, :], psb[:, 0:1024].rearrange("p (k t) -> p k t", k=8),
                  gT[:, :].unsqueeze(2).to_broadcast([128, 8, 128]), ALU.mult, [r_ps, r_g], [r_uT])
            transpose_tile(pg, pb, r_pb, 2, pT, r_pT, "act")

        def stage_b(i):
            tok = slice(i * 128, (i + 1) * 128)
            b = i % 4
            xt, r_xt = xts[i % NX]
            uT, r_uT = uTs[b]
            pT, r_pT = pTs[b]
            for half in range(2):
                cs = slice(half * 512, (half + 1) * 512)
                sg, r_sg = sgs[kkc[0] % 2]
                t1, r_t1 = t1s[kkc[0] % 2]
                kkc[0] += 1
                pgt, r_pgt = next_ps(pg)
                for kc in range(8):
                    pg.mm(pgt[:, :], uT[:, kc, :], wpg[:, kc, cs], kc == 0, kc == 7, [r_uT, r_wpg], [r_pgt])
                ppe, r_ppe = next_ps(pg)
                for kc in range(2):
                    pg.mm(ppe[:, :], pT[:, kc, :], wpl[:, kc, cs], kc == 0, kc == 1, [r_pT, r_wpl], [r_ppe])
                pg.act(sg[:], pgt[:, :], AF.Sigmoid, [r_pgt], [r_sg])
                pg.tt("dve", t1[:], ppe[:, :], sg[:], ALU.mult, [r_ppe, r_sg], [r_t1])
                pg.tt("pool", xt[:, cs], xt[:, cs], t1[:], ALU.add, [r_xt, r_t1], [r_xt])
            cx.dma("sp", pg.xres[tok, :], xt[:], reads=[r_xt], writes=[pg.r_xres[i]])

        NP = NT // 2
        for i in range(min(4, NT)):
            loads(i)
        lockstep([stage_a(0), stage_a(1)])
        for p in range(NP):
            for i in (2 * p + 4, 2 * p + 5):
                if i < NT:
                    loads(i)
            if p + 1 < NP:
                lockstep([stage_a(2 * p + 2), stage_a(2 * p + 3)])
            stage_b(2 * p)
            stage_b(2 * p + 1)
        cx.barrier()


def build(T, depth, debug=False, upto=99):
    pg = Prog(T, depth, debug)
    setup(pg)
    rope_tables(pg)
    for l in range(depth):
        phase1(pg, l)
        if upto <= 1:
            break
        phase2(pg, l)
        if upto <= 2:
            break
        phase3(pg, l)
        if upto <= 3:
            break
        phase4(pg, l)
        if upto <= 4:
            break
        phase5(pg, l)
        if upto <= 5:
            break
        phase6(pg, l)
        if upto <= 6:
            break
        phase7(pg, l)
    pg.cx.finish()
    return pg


def make_in_maps(inputs, T, depth, n_cores):
    hc = host_consts(T)
    maps = []
    for c in range(n_cores):
        b = c % inputs["x"].shape[0]
        m = {"x": np.ascontiguousarray(inputs["x"][b, :T]),
             "p": np.ascontiguousarray(inputs["p"][:depth, b, :T]),
             "positions": np.ascontiguousarray(inputs["positions"][b, :T]).astype(np.int32)}
        for k, v in inputs.items():
            if k in ("x", "p", "positions"):
                continue
            m[k] = np.ascontiguousarray(v[:depth])
        for k, v in hc.items():
            m["c_" + k] = v
        maps.append(m)
    return maps


_CACHE = {}


def kernel(**inputs):
    T, depth, ncores = 4096, 4, 8
    inputs = {k: np.asarray(v) for k, v in inputs.items()}
    if "pg" not in _CACHE:
        _CACHE["pg"] = build(T, depth)
    pg = _CACHE["pg"]
    maps = make_in_maps(inputs, T, depth, ncores)
    res = run_bass_kernel_spmd(pg.nc, maps, core_ids=list(range(ncores)))
    B = inputs["x"].shape[0]
    out = np.stack([np.asarray(res.results[b]["y"], dtype=np.float32) for b in range(B)], 0)
    return out
```

```python
import math
import contextlib
import numpy as np
import ml_dtypes
import concourse.bass as bass
import concourse.mybir as mybir
from concourse.bass_utils import run_bass_kernel_spmd

F32 = mybir.dt.float32
BF16 = mybir.dt.bfloat16
I32 = mybir.dt.int32
AF = mybir.ActivationFunctionType
ALU = mybir.AluOpType
AX = mybir.AxisListType

D = 1024
HD = 64
NEG = -1.0e5
EPS = 1e-6
IN_COLS = 12304
C_MQ, C_MK, C_MV = 0, 512, 1024
C_DQ, C_DK, C_DV = 1536, 3072, 4608
C_Z, C_XBC, C_DT, C_GATE = 6144, 7168, 9216, 9232
FFN = 2816
DIL_RATES = (1, 4, 16)


class R:
    __slots__ = ("name", "lw", "rd")

    def __init__(self, name=""):
        self.name = name
        self.lw = None
        self.rd = {}


class FreshR:
    def __getitem__(self, k):
        return R(k)


class Ctx:
    NDMASEM = 8

    def __init__(self, nc):
        self.nc = nc
        self.eng = {"pe": nc.tensor, "act": nc.scalar, "dve": nc.vector, "pool": nc.gpsimd,
                    "sp": nc.sync}
        self.sems = {}
        self.cnt = {}
        self.seen = {k: {} for k in self.eng}
        self._cms = []
        for k in ("pe", "act", "dve", "pool"):
            self._mksem(k)
            self.cnt[k] = 0
        self.dmacnt = {"sp": 0, "pool": 0}
        for q in ("sp", "pool"):
            for j in range(self.NDMASEM):
                self._mksem(f"dma_{q}_{j}")
        self.n_instr = 0
        self.n_wait = 0

    def _mksem(self, key):
        cm = self.nc.semaphore(key)
        h = cm.__enter__()
        self._cms.append(cm)
        self.sems[key] = h

    def close(self):
        for cm in reversed(self._cms):
            cm.__exit__(None, None, None)

    @staticmethod
    def _need(deps, sk, val):
        if val > deps.get(sk, 0):
            deps[sk] = val

    def _emit_waits(self, ek, deps):
        e = self.eng[ek]
        seen = self.seen[ek]
        for sk, val in deps.items():
            if seen.get(sk, 0) >= val:
                continue
            e.wait_ge(self.sems[sk], val)
            seen[sk] = val
            self.n_wait += 1

    def _deps(self, ek, reads, writes, is_dma):
        deps = {}
        for r in reads:
            if r.lw is not None:
                self._need(deps, *r.lw)
        for w in writes:
            if w.lw is not None and (is_dma or w.lw[0] != ek):
                self._need(deps, *w.lw)
            for sk, val in w.rd.items():
                if is_dma or sk != ek:
                    self._need(deps, sk, val)
        return deps

    def _record(self, tok, reads, writes):
        sk, val = tok
        for r in reads:
            if val > r.rd.get(sk, 0):
                r.rd[sk] = val
        for w in writes:
            w.lw = tok
            w.rd = {}

    def op(self, ek, fn, reads=(), writes=()):
        deps = self._deps(ek, reads, writes, False)
        self._emit_waits(ek, deps)
        ins = fn()
        self.cnt[ek] += 1
        ins.then_inc(self.sems[ek], 1)
        self.n_instr += 1
        self._record((ek, self.cnt[ek]), reads, writes)
        return ins

    def dma(self, q, out, in_, reads=(), writes=(), **kw):
        i = self.dmacnt[q]
        j = i % self.NDMASEM
        sk = f"dma_{q}_{j}"
        deps = self._deps(q, reads, writes, True)
        if i >= self.NDMASEM:
            self._need(deps, sk, 16 * (i // self.NDMASEM))
        self._emit_waits(q, deps)
        ins = self.eng[q].dma_start(out=out, in_=in_, **kw)
        val = 16 * (i // self.NDMASEM + 1)
        ins.then_inc(self.sems[sk], 16)
        self.dmacnt[q] += 1
        self.n_instr += 1
        self._record((sk, val), reads, writes)
        return ins

    def _all_tokens(self):
        deps = {}
        for k in ("pe", "act", "dve", "pool"):
            if self.cnt[k] > 0:
                deps[k] = self.cnt[k]
        K = self.NDMASEM
        for q in ("sp", "pool"):
            n = self.dmacnt[q]
            for j in range(K):
                c = (n - j + K - 1) // K if n > j else 0
                if c > 0:
                    deps[f"dma_{q}_{j}"] = 16 * c
        return deps

    def barrier(self):
        deps = self._all_tokens()
        for ek in ("pe", "act", "dve", "pool", "sp"):
            self._emit_waits(ek, dict(deps))

    def finish(self):
        self._emit_waits("sp", self._all_tokens())


class Prog:
    def __init__(self, T, depth, debug=False):
        self.T = T
        self.NT = T // 128
        self.depth = depth
        self.debug = debug
        self.nc = bass.Bass("TRN2", target_bir_lowering=False)
        self.cx = Ctx(self.nc)
        self.es = contextlib.ExitStack()
        self.dram_in = {}
        self.uid = 0

    def din(self, name, shape, dt=F32):
        t = self.nc.dram_tensor(name, list(shape), dt, kind="ExternalInput").ap()
        self.dram_in[name] = t
        return t

    def dscr(self, name, shape, dt, out=False):
        kind = "ExternalOutput" if (out or self.debug) else "Internal"
        return self.nc.dram_tensor(name, list(shape), dt, kind=kind).ap()

    def sb(self, stack, shape, dt, name=None):
        self.uid += 1
        name = f"{name or 't'}_{self.uid}"
        t = stack.enter_context(self.nc.sbuf_tensor(name, list(shape), dt))
        return t, R(name)

    def sbn(self, stack, n, shape, dt, name=None):
        return [self.sb(stack, shape, dt, name) for _ in range(n)]

    def act(self, out, in_, func, reads, writes, **kw):
        return self.cx.op("act", lambda: self.nc.scalar.activation(out=out, in_=in_, func=func, **kw),
                          reads, writes)

    def ts(self, ek, out, in0, s1, s2, op0, op1, reads, writes):
        e = self.cx.eng[ek]
        if op1 is None:
            return self.cx.op(ek, lambda: e.tensor_scalar(out=out, in0=in0, scalar1=s1, scalar2=None, op0=op0),
                              reads, writes)
        return self.cx.op(ek, lambda: e.tensor_scalar(out=out, in0=in0, scalar1=s1, scalar2=s2, op0=op0, op1=op1),
                          reads, writes)

    def tt(self, ek, out, in0, in1, op, reads, writes):
        e = self.cx.eng[ek]
        return self.cx.op(ek, lambda: e.tensor_tensor(out=out, in0=in0, in1=in1, op=op), reads, writes)

    def cp(self, ek, out, in_, reads, writes):
        e = self.cx.eng[ek]
        if ek == "act":
            return self.cx.op(ek, lambda: e.copy(out=out, in_=in_), reads, writes)
        return self.cx.op(ek, lambda: e.tensor_copy(out=out, in_=in_), reads, writes)

    def mm(self, out, lhsT, rhs, start, stop, reads, writes):
        return self.cx.op("pe", lambda: self.nc.tensor.matmul(out, lhsT, rhs, start=start, stop=stop),
                          reads, writes)

    def tr(self, out, in_, ident, reads, writes):
        return self.cx.op("pe", lambda: self.nc.tensor.transpose(out, in_, ident), reads, writes)


def host_consts(T=4096):
    c = {}
    c["ident_bf"] = np.eye(128, dtype=np.float32).astype(ml_dtypes.bfloat16)
    c["ident_f"] = np.eye(128, dtype=np.float32)
    inv = 500000.0 ** (-np.arange(0, 16, 2, dtype=np.float32) / 16.0)
    c["invf"] = np.tile(inv.astype(np.float32)[None, :], (128, 1))
    k = np.arange(128)[:, None]
    q = np.arange(512)[None, :]
    tb = np.stack([np.where(j * 128 + k <= q, 0.0, NEG) for j in range(4)], 0)
    c["tb_moba"] = tb.transpose(1, 0, 2).astype(ml_dtypes.bfloat16).copy()
    q1 = np.arange(128)[None, :]
    cur = np.where(k <= q1, 0.0, NEG)
    prev = np.where(k >= q1, 0.0, NEG)
    c["tb_dil"] = np.concatenate([cur, prev], 1).astype(ml_dtypes.bfloat16)
    c["tri_f"] = (k <= q1).astype(np.float32)
    NT = T // 128
    n = np.arange(16)[None, :]
    jb = (np.arange(NT) // 2)[:, None]
    c["pm"] = np.tile(np.where(n < jb, 0.0, -1e30).astype(np.float32)[None], (128, 1, 1))
    c["own"] = np.tile((n == jb).astype(np.float32)[None], (128, 1, 1))
    c["ind"] = (np.arange(16)[:, None] == (np.arange(T) // 256)[None, :]).astype(np.float32).astype(ml_dtypes.bfloat16)
    oh = np.zeros((16, 16, 128), np.float32)
    for h in range(16):
        oh[h, h, :] = 1.0
    c["oneh"] = oh.reshape(16, 2048)
    c["ones_f"] = np.ones((128, 128), np.float32)
    return c


def setup(pg):
    nc, cx, T, NT, L = pg.nc, pg.cx, pg.T, pg.NT, pg.depth
    P = pg
    P.x_in = P.din("x", [T, D])
    P.p_in = P.din("p", [L, T, 256])
    P.pos_in = P.din("positions", [T], I32)
    P.w = {}
    for name, shape in [
        ("norm_mix", [L, D]), ("w_in", [L, D, IN_COLS]), ("b_gate", [L, 3 * D]),
        ("moba_q_norm", [L, HD]), ("moba_k_norm", [L, HD]), ("dil_q_norm", [L, HD]), ("dil_k_norm", [L, HD]),
        ("ssm_conv_w", [L, 4, 2048]), ("ssm_conv_b", [L, 2048]), ("ssm_dt_bias", [L, 16]),
        ("ssm_a_log", [L, 16]), ("ssm_d", [L, 16]), ("ssm_out_norm", [L, D]),
        ("w_br_moba", [L, 512, D]), ("w_br_ssm", [L, D, D]), ("w_br_dil", [L, 512, D]),
        ("w_out", [L, D, D]), ("norm_ffn", [L, D]), ("w_up", [L, D, 2 * FFN]),
        ("ffn_conv_w", [L, 3, 2 * FFN]), ("ffn_conv_b", [L, 2 * FFN]), ("w_down", [L, FFN, D]),
        ("norm_ple", [L, D]), ("w_ple_gate", [L, D, D]), ("w_ple", [L, 256, D]),
    ]:
        P.w[name] = P.din(name, shape)
    hc = host_consts(T)
    P.c_in = {}
    for k, v in hc.items():
        P.c_in[k] = P.din("c_" + k, v.shape, BF16 if v.dtype == ml_dtypes.bfloat16 else F32)
    P.xres = nc.dram_tensor("y", [T, D], F32, kind="ExternalOutput").ap()
    P.r_xres = [R(f"xres{i}") for i in range(NT)]
    P.s_mq = P.dscr("s_mq", [T, 512], BF16)
    P.s_mk = P.dscr("s_mk", [T, 512], BF16)
    P.s_mv = P.dscr("s_mv", [T, 512], BF16)
    P.s_dq = P.dscr("s_dq", [3, T, 512], BF16)
    P.s_dk = P.dscr("s_dk", [3, T, 512], BF16)
    P.s_dv = P.dscr("s_dv", [3, T, 512], BF16)
    P.s_sz = P.dscr("s_sz", [T, D], F32)
    P.s_gate = P.dscr("s_gate", [T, 3 * D], BF16)
    P.s_xsT = P.dscr("s_xsT", [D, T], F32)
    P.s_bcT = P.dscr("s_bcT", [D, T], BF16)
    P.s_oa = P.dscr("s_oa", [T, 512], BF16)
    P.s_ob = P.dscr("s_ob", [T, D], BF16)
    P.s_oc = P.dscr("s_oc", [T, 512], BF16)
    P.s_nd = P.dscr("s_nd", [3, T, 8 * 65], F32)
    P.r_scr = FreshR()
    g = pg.es
    P.ident_bf, P.r_const = P.sb(g, [128, 128], BF16, "identbf")
    P.ident_f, _ = P.sb(g, [128, 128], F32, "identf")
    P.invf, _ = P.sb(g, [128, 8], F32, "invf")
    P.tri_f, _ = P.sb(g, [128, 128], F32, "trif")
    rc = P.r_const
    for t, k in [(P.ident_bf, "ident_bf"), (P.ident_f, "ident_f"), (P.invf, "invf"), (P.tri_f, "tri_f")]:
        cx.dma("sp", t[:], P.c_in[k], writes=[rc])
    P.dt_all, P.r_dt = P.sb(g, [128, NT, 16], F32, "dtall")
    P.cos, P.r_rope = P.sb(g, [128, NT, 8], F32, "cos")
    P.sin, _ = P.sb(g, [128, NT, 8], F32, "sin")
    P.ps = []
    for i in range(8):
        t = g.enter_context(nc.psum_tensor(f"psb{i}", [128, 512], F32))
        P.ps.append((t, R(f"ps{i}")))
    P.ps_rr = 0
    P.ps_pool = 6


def next_ps(pg):
    t = pg.ps[pg.ps_rr % pg.ps_pool]
    pg.ps_rr += 1
    return t


def rope_tables(pg):
    nc, cx, NT = pg.nc, pg.cx, pg.NT
    PI = math.pi
    with contextlib.ExitStack() as st:
        posi, r0 = pg.sb(st, [128, NT], I32, "posi")
        posf, r1 = pg.sb(st, [128, NT], F32, "posf")
        ang, r2 = pg.sb(st, [128, NT, 8], F32, "ang")
        kf, r3 = pg.sb(st, [128, NT, 8], F32, "kf")
        ki, r4 = pg.sb(st, [128, NT, 8], I32, "ki")
        m, r5 = pg.sb(st, [128, NT, 8], F32, "m")
        a2, r6 = pg.sb(st, [128, NT, 8], F32, "a2")
        with nc.allow_non_contiguous_dma(reason="tiny positions load"):
            cx.dma("sp", posi[:], pg.pos_in.rearrange("(n p) -> p n", p=128), writes=[r0])
        pg.cp("dve", posf[:], posi[:], [r0], [r1])
        pg.tt("dve", ang[:], posf[:, :].unsqueeze(2).to_broadcast([128, NT, 8]),
              pg.invf[:, :].unsqueeze(1).to_broadcast([128, NT, 8]), ALU.mult, [r1, pg.r_const], [r2])
        pg.ts("dve", kf[:], ang[:], 1.0 / (2 * PI), None, ALU.mult, None, [r2], [r3])
        pg.cp("dve", ki[:], kf[:], [r3], [r4])
        pg.cp("dve", kf[:], ki[:], [r4], [r3])
        pg.ts("dve", kf[:], kf[:], -2 * PI, None, ALU.mult, None, [r3], [r3])
        pg.tt("dve", ang[:], ang[:], kf[:], ALU.add, [r2, r3], [r2])

        def wrap(t, rt):
            pg.ts("dve", m[:], t[:], PI, -2 * PI, ALU.is_gt, ALU.mult, [rt], [r5])
            pg.tt("dve", t[:], t[:], m[:], ALU.add, [rt, r5], [rt])
            pg.ts("dve", m[:], t[:], -PI, 2 * PI, ALU.is_lt, ALU.mult, [rt], [r5])
            pg.tt("dve", t[:], t[:], m[:], ALU.add, [rt, r5], [rt])

        wrap(ang, r2)
        pg.ts("dve", a2[:], ang[:], PI / 2, None, ALU.add, None, [r2], [r6])
        wrap(a2, r6)
        pg.act(pg.sin[:], ang[:], AF.Sin, [r2], [pg.r_rope])
        pg.act(pg.cos[:], a2[:], AF.Sin, [r6], [pg.r_rope])
        cx.barrier()


def lockstep(gens):
    gens = list(gens)
    while gens:
        for g in list(gens):
            try:
                next(g)
            except StopIteration:
                gens.remove(g)


def rms_to_featmajor(pg, st, x_src, r_src, gamma_ap, uT, r_uT, nm):
    nc, cx, NT = pg.nc, pg.cx, pg.NT
    gT, r_g = pg.sb(st, [128, 8], F32, nm + "gT")
    with nc.allow_non_contiguous_dma(reason="small gamma load"):
        cx.dma("sp", gT[:], gamma_ap.rearrange("(kc p) -> p kc", p=128), writes=[r_g])
    G = 3
    with contextlib.ExitStack() as st2:
        xts = pg.sbn(st2, 2 * G, [128, D], F32, nm + "xt")
        xns = pg.sbn(st2, 2 * G, [128, D], BF16, nm + "xn")
        sss = pg.sbn(st2, 2 * G, [128, 1], F32, nm + "ss")

        def tile_gen(i):
            xt, r_xt = xts[i % (2 * G)]
            xn, r_xn = xns[i % (2 * G)]
            ss, r_ss = sss[i % (2 * G)]
            pg.act(xn[:], xt[:], AF.Square, [r_xt], [r_xn, r_ss], accum_out=ss[:, 0:1])
            yield
            pg.ts("dve", ss[:], ss[:], 1.0 / D, EPS, ALU.mult, ALU.add, [r_ss], [r_ss])
            yield
            pg.act(ss[:], ss[:], AF.Sqrt, [r_ss], [r_ss])
            yield
            pg.cx.op("dve", lambda: nc.vector.reciprocal(out=ss[:], in_=ss[:]), [r_ss], [r_ss])
            yield
            pg.ts("dve", xn[:], xt[:], ss[:, 0:1], None, ALU.mult, None, [r_xt, r_ss], [r_xn])
            yield
            pst, r_ps = next_ps(pg)
            psb = pst[:, :].bitcast(BF16)
            for kc in range(8):
                pg.tr(psb[:, kc * 128:(kc + 1) * 128], xn[:, kc * 128:(kc + 1) * 128], pg.ident_bf[:, :],
                      [r_xn, pg.r_const], [r_ps])
            yield
            pg.tt("dve", uT[:, :, i * 128:(i + 1) * 128], psb[:, 0:1024].rearrange("p (k t) -> p k t", k=8),
                  gT[:, :].unsqueeze(2).to_broadcast([128, 8, 128]), ALU.mult, [r_ps, r_g], [r_uT[i]])

        def load(i):
            xt, r_xt = xts[i % (2 * G)]
            cx.dma("sp", xt[:], x_src[i * 128:(i + 1) * 128, :], reads=[r_src[i]] if r_src else [], writes=[r_xt])

        for i in range(min(2 * G, NT)):
            load(i)
        for i0 in range(0, NT, G):
            lockstep([tile_gen(i) for i in range(i0, min(NT, i0 + G))])
            for i in range(i0 + 2 * G, min(NT, i0 + 3 * G)):
                load(i)
        cx.barrier()


def phase1(pg, l):
    nc, cx, T, NT = pg.nc, pg.cx, pg.T, pg.NT
    W = pg.w
    x_src = pg.x_in if l == 0 else pg.xres
    r_src = None if l == 0 else pg.r_xres
    RS = pg.r_scr
    with contextlib.ExitStack() as st:
        uT, _ = pg.sb(st, [128, 8, T], BF16, "uT")
        r_uT = [R(f"uT{i}") for i in range(NT)]
        rms_to_featmajor(pg, st, x_src, r_src, W["norm_mix"][l], uT, r_uT, "p1")

        def bc_load(src_ap, n, nm):
            t, r = pg.sb(st, [128, n], F32, nm)
            cx.dma("sp", t[:], src_ap.partition_broadcast(128), writes=[r])
            return t, r
        g_mq, r_gmq = bc_load(W["moba_q_norm"][l], HD, "gmq")
        g_mk, r_gmk = bc_load(W["moba_k_norm"][l], HD, "gmk")
        g_dq, r_gdq = bc_load(W["dil_q_norm"][l], HD, "gdq")
        g_dk, r_gdk = bc_load(W["dil_k_norm"][l], HD, "gdk")
        bgate, r_bg = bc_load(W["b_gate"][l], 3 * D, "bgate")
        dtb, r_dtb = bc_load(W["ssm_dt_bias"][l], 16, "dtb")
        cw, r_cw = pg.sb(st, [128, 16, 4], F32, "cw")
        cb, r_cb = pg.sb(st, [128, 16], F32, "cb")
        with nc.allow_non_contiguous_dma(reason="small conv weight load"):
            for k in range(4):
                cx.dma("sp", cw[:, :, k], W["ssm_conv_w"][l][k].rearrange("(cc p) -> p cc", p=128), writes=[r_cw])
            cx.dma("sp", cb[:], W["ssm_conv_b"][l].rearrange("(cc p) -> p cc", p=128), writes=[r_cb])

        def rope_tabs(gt, r_gt, nm):
            ta, r_t = pg.sb(st, [128, NT, 16], F32, nm + "A")
            tb, _ = pg.sb(st, [128, NT, 16], F32, nm + "B")
            for half in range(2):
                gb = gt[:, half * 8:(half + 1) * 8].unsqueeze(1).to_broadcast([128, NT, 8])
                pg.tt("pool", ta[:, :, half * 8:(half + 1) * 8], pg.cos[:, :, :], gb, ALU.mult, [pg.r_rope, r_gt], [r_t])
                pg.tt("pool", tb[:, :, half * 8:(half + 1) * 8], pg.sin[:, :, :], gb, ALU.mult, [pg.r_rope, r_gt], [r_t])
            return ta, tb, r_t
        rtab = {}
        for nm, (gt, r_gt) in (("mq", (g_mq, r_gmq)), ("mk", (g_mk, r_gmk)), ("dq", (g_dq, r_gdq)), ("dk", (g_dk, r_gdk))):
            rtab[id(gt)] = rope_tabs(gt, r_gt, "rt" + nm)
        wbs = pg.sbn(st, 2, [128, 8, 512], BF16, "wb")
        wctr = [0]

        def load_w(c0, ncols):
            wb, r_wb = wbs[wctr[0] % 2]
            wctr[0] += 1
            cx.dma("pool", wb[:, :, 0:ncols], W["w_in"][l][:, c0:c0 + ncols].rearrange("(kc p) c -> p kc c", p=128),
                   writes=[r_wb])
            return wb, r_wb

        groups = [(C_MQ, "qk", (g_mq, r_gmq, pg.s_mq, RS["mq"])),
                  (C_MK, "qk", (g_mk, r_gmk, pg.s_mk, RS["mk"])),
                  (C_MV, "v", (pg.s_mv, RS["mv"]))]
        for g in range(3):
            groups.append((C_DQ + g * 512, "qk", (g_dq, r_gdq, pg.s_dq[g], RS["dq"])))
            groups.append((C_DK + g * 512, "qk", (g_dk, r_gdk, pg.s_dk[g], RS["dk"])))
            groups.append((C_DV + g * 512, "v", (pg.s_dv[g], RS["dv"])))
        for j in range(2):
            groups.append((C_Z + j * 512, "z", (j,)))
        for j in range(6):
            groups.append((C_GATE + j * 512, "gate", (j,)))

        sqs = pg.sbn(st, 6, [128, 512], F32, "sq")
        ys = pg.sbn(st, 6, [128, 512], F32, "y")
        obs = pg.sbn(st, 8, [128, 512], BF16, "ob")
        ofs = pg.sbn(st, 2, [128, 512], F32, "of")
        smalls = pg.sbn(st, 6, [128, 8], F32, "ssq")
        rts = pg.sbn(st, 6, [128, 2, 8, 16], F32, "ropet")
        it = [0]

        def post_qk(pst, r_ps, i, gt, r_gt, dest, r_dest):
            k = it[0]
            it[0] += 1
            sq, r_sq = sqs[k % 6]
            y, r_y = ys[k % 6]
            ob, r_ob = obs[k % 8]
            ss, r_ss = smalls[k % 6]
            rt, r_rt = rts[k % 6]
            pg.act(sq[:], pst[:, :], AF.Square, [r_ps], [r_sq])
            yield
            pg.cx.op("dve", lambda: nc.vector.tensor_reduce(out=ss[:], in_=sq[:, :].rearrange("p (h d) -> p h d", h=8),
                                                           axis=AX.X, op=ALU.add), [r_sq], [r_ss])
            pg.ts("dve", ss[:], ss[:], 1.0 / HD, EPS, ALU.mult, ALU.add, [r_ss], [r_ss])
            yield
            pg.act(ss[:], ss[:], AF.Sqrt, [r_ss], [r_ss])
            yield
            pg.cx.op("dve", lambda: nc.vector.reciprocal(out=ss[:], in_=ss[:]), [r_ss], [r_ss])
            y3 = y[:, :].rearrange("p (h d) -> p h d", h=8)
            pg.tt("dve", y3, pst[:, :].rearrange("p (h d) -> p h d", h=8),
                  ss[:, :].unsqueeze(2).to_broadcast([128, 8, HD]), ALU.mult, [r_ps, r_ss], [r_y])
            yield
            ob3 = ob[:, :].rearrange("p (h d) -> p h d", h=8)
            pg.tt("dve", ob3[:, :, 16:64], y3[:, :, 16:64], gt[:, 16:64].unsqueeze(1).to_broadcast([128, 8, 48]),
                  ALU.mult, [r_y, r_gt], [r_ob])
            ta, tb, r_t = rtab[id(gt)]
            y16 = y3[:, :, 0:16]
            pg.tt("pool", rt[:, 0], y16, ta[:, i, :].unsqueeze(1).to_broadcast([128, 8, 16]), ALU.mult,
                  [r_y, r_t], [r_rt])
            pg.tt("pool", rt[:, 1], y16, tb[:, i, :].unsqueeze(1).to_broadcast([128, 8, 16]), ALU.mult,
                  [r_y, r_t], [r_rt])
            yield
            pg.tt("pool", ob3[:, :, 0:8], rt[:, 0, :, 0:8], rt[:, 1, :, 8:16], ALU.subtract, [r_rt], [r_ob])
            pg.tt("pool", ob3[:, :, 8:16], rt[:, 0, :, 8:16], rt[:, 1, :, 0:8], ALU.add, [r_rt], [r_ob])
            yield
            cx.dma("sp", dest[i * 128:(i + 1) * 128, :], ob[:], reads=[r_ob], writes=[r_dest])

        for gi, (c0, kind, info) in enumerate(groups):
            wb, r_wb = load_w(c0, 512)
            if kind == "qk":
                GQ = 3
                for i0 in range(0, NT, GQ):
                    gens = []
                    for i in range(i0, min(NT, i0 + GQ)):
                        pst, r_ps = next_ps(pg)
                        for kc in range(8):
                            pg.mm(pst[:, :], uT[:, kc, i * 128:(i + 1) * 128], wb[:, kc, :], kc == 0, kc == 7,
                                  [r_uT[i], r_wb], [r_ps])
                        gens.append(post_qk(pst, r_ps, i, *info))
                    lockstep(gens)
                continue
            for i in range(NT):
                pst, r_ps = next_ps(pg)
                for kc in range(8):
                    pg.mm(pst[:, :], uT[:, kc, i * 128:(i + 1) * 128], wb[:, kc, :], kc == 0, kc == 7,
                          [r_uT[i], r_wb], [r_ps])
                if kind == "qk":
                    post_qk(pst, r_ps, i, *info)
                elif kind == "v":
                    k = it[0]
                    it[0] += 1
                    ob, r_ob = obs[k % 3]
                    pg.cp("act", ob[:], pst[:, :], [r_ps], [r_ob])
                    cx.dma("sp", info[0][i * 128:(i + 1) * 128, :], ob[:], reads=[r_ob], writes=[info[1]])
                elif kind == "z":
                    k = it[0]
                    it[0] += 1
                    of, r_of = ofs[k % 2]
                    pg.act(of[:], pst[:, :], AF.Silu, [r_ps], [r_of])
                    j = info[0]
                    cx.dma("sp", pg.s_sz[i * 128:(i + 1) * 128, j * 512:(j + 1) * 512], of[:], reads=[r_of],
                           writes=[RS["sz"]])
                elif kind == "gate":
                    k = it[0]
                    it[0] += 1
                    of, r_of = ofs[k % 2]
                    ob, r_ob = obs[k % 3]
                    j = info[0]
                    pg.tt("dve", of[:], pst[:, :], bgate[:, j * 512:(j + 1) * 512], ALU.add, [r_ps, r_bg], [r_of])
                    pg.act(ob[:], of[:], AF.Sigmoid, [r_of], [r_ob])
                    cx.dma("sp", pg.s_gate[i * 128:(i + 1) * 128, j * 512:(j + 1) * 512], ob[:], reads=[r_ob],
                           writes=[RS["gate"]])

        wb, r_wb = load_w(C_DT, 16)
        dts = pg.sbn(st, 2, [128, 16], F32, "dtt")
        for i in range(NT):
            pst, r_ps = next_ps(pg)
            for kc in range(8):
                pg.mm(pst[:, 0:16], uT[:, kc, i * 128:(i + 1) * 128], wb[:, kc, 0:16], kc == 0, kc == 7,
                      [r_uT[i], r_wb], [r_ps])
            dt_, r_dt = dts[i % 2]
            pg.tt("dve", dt_[:], pst[:, 0:16], dtb[:, :], ALU.add, [r_ps, r_dtb], [r_dt])
            pg.act(dt_[:], dt_[:], AF.Exp, [r_dt], [r_dt])
            pg.act(pg.dt_all[:, i, :], dt_[:], AF.Ln, [r_dt], [pg.r_dt], bias=1.0)

        raws = pg.sbn(st, 4, [128, 3 + 512], F32, "raw")
        accs = pg.sbn(st, 4, [128, 512], F32, "acc")
        NG = T // 512
        k = 0
        for cg in range(4):
            wb, r_wb = load_w(C_XBC + cg * 512, 512)
            for cc in range(4):
                ch = cg * 4 + cc
                for tg in range(NG):
                    pst, r_ps = next_ps(pg)
                    for kc in range(8):
                        pg.mm(pst[:, :], wb[:, kc, cc * 128:(cc + 1) * 128], uT[:, kc, tg * 512:(tg + 1) * 512],
                              kc == 0, kc == 7, [r_uT[tg * 4 + j] for j in range(4)] + [r_wb], [r_ps])
                    raw, r_raw = raws[k % 4]
                    praw, r_praw = raws[(k - 1) % 4]
                    acc, r_acc = accs[k % 4]
                    if tg == 0:
                        pg.cx.op("pool", lambda: nc.gpsimd.memset(raw[:, 0:3], 0.0), [], [r_raw])
                    else:
                        pg.cp("pool", raw[:, 0:3], praw[:, 512:515], [r_praw], [r_raw])
                    pg.cp("act", raw[:, 3:515], pst[:, :], [r_ps], [r_raw])
                    pg.ts("dve", acc[:], raw[:, 0:512], cw[:, ch, 0:1], cb[:, ch:ch + 1], ALU.mult, ALU.add,
                          [r_raw, r_cw, r_cb], [r_acc])
                    for kk in range(1, 4):
                        pg.cx.op("dve", lambda kk=kk: nc.vector.scalar_tensor_tensor(
                            out=acc[:], in0=raw[:, kk:kk + 512], scalar=cw[:, ch, kk:kk + 1], in1=acc[:],
                            op0=ALU.mult, op1=ALU.add), [r_raw, r_cw, r_acc], [r_acc])
                    if ch < 8:
                        of, r_of = ofs[k % 2]
                        pg.act(of[:], acc[:], AF.Silu, [r_acc], [r_of])
                        cx.dma("sp", pg.s_xsT[ch * 128:(ch + 1) * 128, tg * 512:(tg + 1) * 512], of[:],
                               reads=[r_of], writes=[RS["xsT"]])
                    else:
                        ob, r_ob = obs[k % 3]
                        pg.act(ob[:], acc[:], AF.Silu, [r_acc], [r_ob])
                        cx.dma("sp", pg.s_bcT[(ch - 8) * 128:(ch - 7) * 128, tg * 512:(tg + 1) * 512], ob[:],
                               reads=[r_ob], writes=[RS["bcT"]])
                    k += 1
        cx.barrier()


def phase2(pg, l):
    nc, cx, T, NT = pg.nc, pg.cx, pg.T, pg.NT
    NS = T // 512
    with contextlib.ExitStack() as st:
        oa, _ = pg.sb(st, [128, NT, 512], BF16, "oa")
        r_oa = [R() for _ in range(NT)]
        pg.pm, r_c2 = pg.sb(st, [128, NT, 16], F32, "pm")
        pg.own, _ = pg.sb(st, [128, NT, 16], F32, "own")
        pg.tb_moba, _ = pg.sb(st, [128, 4, 512], BF16, "tbmoba")
        cx.dma("sp", pg.pm[:], pg.c_in["pm"], writes=[r_c2])
        cx.dma("sp", pg.own[:], pg.c_in["own"], writes=[r_c2])
        cx.dma("sp", pg.tb_moba[:], pg.c_in["tb_moba"], writes=[r_c2])
        kTx, r_kT = pg.sb(st, [80, T], BF16, "kTx")
        qTx, r_qT = pg.sb(st, [80, T], BF16, "qTx")
        cx.dma("sp", kTx[64:80, :], pg.c_in["ind"], writes=[r_kT])
        ktm, r_ktm = pg.sb(st, [128, NT, 64], BF16, "ktm")
        qtm, r_qtm = pg.sb(st, [128, NT, 64], BF16, "qtm")
        vext, r_v = pg.sb(st, [128, NT, 65], BF16, "vext")
        cx.op("pool", lambda: nc.gpsimd.memset(vext[:, :, 64:65], 1.0), [], [r_v])
        kmf, r_kmf = pg.sb(st, [64, 16], F32, "kmf")
        kmb, r_kmb = pg.sb(st, [64, 16], BF16, "kmb")
        cx.op("pool", lambda: nc.gpsimd.memset(kmf[:], 0.0), [], [r_kmf])
        scm, r_scm = pg.sb(st, [128, NT, 16], F32, "scm")
        m8, r_m8 = pg.sb(st, [128, NT, 8], F32, "m8")
        thr, r_thr = pg.sb(st, [128, NT, 1], F32, "thr")
        sel, r_sel = pg.sb(st, [128, NT, 16], F32, "sel")
        btm, r_btm = pg.sb(st, [128, NT, 16], BF16, "btm")
        es = pg.sbn(st, 4, [128, 512], BF16, "e")
        ectr = [0]
        recs = pg.sbn(st, 2, [128, 4], F32, "rec")
        NB = T // 256
        with nc.allow_non_contiguous_dma(reason="per-head 128B rows"):
            for h in range(8):
                hs = slice(h * 64, (h + 1) * 64)
                cx.dma("sp", ktm[:], pg.s_mk[:, hs].rearrange("(n p) d -> p n d", p=128), writes=[r_ktm])
                cx.dma("sp", qtm[:], pg.s_mq[:, hs].rearrange("(n p) d -> p n d", p=128), writes=[r_qtm])
                cx.dma("sp", vext[:, :, 0:64], pg.s_mv[:, hs].rearrange("(n p) d -> p n d", p=128), writes=[r_v])
                for (src, r_src, dst, r_dst) in ((ktm, r_ktm, kTx, r_kT), (qtm, r_qtm, qTx, r_qT)):
                    for i0 in range(0, NT, 8):
                        n = min(8, NT - i0)
                        pst, r_ps = next_ps(pg)
                        psb = pst[:, :].bitcast(BF16)
                        for i in range(n):
                            pg.tr(psb[0:64, i * 128:(i + 1) * 128], src[:, i0 + i, :], pg.ident_bf[:, :],
                                  [r_src, pg.r_const], [r_ps])
                        pg.cp("dve", dst[0:64, i0 * 128:(i0 + n) * 128], psb[0:64, 0:n * 128], [r_ps], [r_dst])
                cx.op("dve", lambda: nc.vector.tensor_reduce(
                    out=kmf[:, 0:NB], in_=kTx[0:64, :].rearrange("p (n k) -> p n k", k=256), axis=AX.X, op=ALU.add),
                    [r_kT], [r_kmf])
                pg.ts("dve", kmb[:], kmf[:], 1.0 / 256, None, ALU.mult, None, [r_kmf], [r_kmb])
                for i in range(NT):
                    pst, r_ps = next_ps(pg)
                    pg.mm(pst[:, 0:16], qTx[0:64, i * 128:(i + 1) * 128], kmb[:, :], True, True, [r_qT, r_kmb], [r_ps])
                    pg.tt("dve", scm[:, i, :], pst[:, 0:16], pg.pm[:, i, :], ALU.add, [r_ps, r_c2], [r_scm])
                    cx.op("dve", lambda i=i: nc.vector.max(out=m8[:, i, :], in_=scm[:, i, :]), [r_scm], [r_m8])
                pg.ts("dve", thr[:], m8[:, :, 2:3], -1e29, None, ALU.max, None, [r_m8], [r_thr])
                pg.tt("dve", sel[:], scm[:], thr[:, :, :].to_broadcast([128, NT, 16]), ALU.is_ge, [r_scm, r_thr], [r_sel])
                pg.tt("dve", sel[:], sel[:], pg.own[:], ALU.add, [r_sel, r_c2], [r_sel])
                pg.ts("dve", btm[:], sel[:], -1.0, -NEG, ALU.add, ALU.mult, [r_sel], [r_btm])
                for i0 in range(0, NT, 8):
                    n = min(8, NT - i0)
                    pst, r_ps = next_ps(pg)
                    psb = pst[:, :].bitcast(BF16)
                    for i in range(n):
                        pg.tr(psb[64:80, i * 128:(i + 1) * 128], btm[:, i0 + i, :], pg.ident_bf[:, :],
                              [r_btm, pg.r_const], [r_ps])
                    pg.cp("dve", qTx[64:80, i0 * 128:(i0 + n) * 128], psb[64:80, 0:n * 128], [r_ps], [r_qT])
                for m in range(NS):
                    po, r_po = pg.ps[6 + (m % 2)]
                    nj = 4 * m + 4

                    def scores(j):
                        pst, r_ps = next_ps(pg)
                        diag = j >= 4 * m
                        pg.mm(pst[:, :], kTx[0:80, j * 128:(j + 1) * 128], qTx[0:80, m * 512:(m + 1) * 512],
                              True, not diag, [r_kT, r_qT], [r_ps])
                        if diag:
                            pg.mm(pst[:, :], pg.ident_bf[:, :], pg.tb_moba[:, j - 4 * m, :], False, True,
                                  [pg.r_const, r_c2], [r_ps])
                        e, r_e = es[ectr[0] % 4]
                        ectr[0] += 1
                        pg.act(e[:], pst[:, :], AF.Exp, [r_ps], [r_e], scale=0.125)
                        return e, r_e

                    def pv(j, e, r_e):
                        for qi in range(4):
                            if j > 4 * m + qi:
                                continue
                            pg.mm(po[:, qi * 65:(qi + 1) * 65], e[:, qi * 128:(qi + 1) * 128], vext[:, j, :],
                                  (j == 0 and qi == 0), (j == 4 * m + qi), [r_e, r_v], [r_po])

                    pend = [(0,) + scores(0)]
                    for j in range(1, nj):
                        pend.append((j,) + scores(j))
                        if len(pend) > 2:
                            pv(*pend.pop(0))
                    while pend:
                        pv(*pend.pop(0))
                    rec, r_rec = recs[m % 2]
                    po3 = po[:, 0:260].rearrange("p (q d) -> p q d", d=65)
                    cx.op("dve", lambda: nc.vector.reciprocal(out=rec[:], in_=po3[:, :, 64]), [r_po], [r_rec])
                    for qi in range(4):
                        pg.ts("dve", oa[:, 4 * m + qi, hs], po[:, qi * 65:qi * 65 + 64], rec[:, qi:qi + 1], None,
                              ALU.mult, None, [r_po, r_rec], [r_oa[4 * m + qi]])
        cx.dma("sp", pg.s_oa.rearrange("(n p) d -> p n d", p=128), oa[:], reads=r_oa)
        cx.barrier()


def phase3(pg, l):
    nc, cx, T, NT = pg.nc, pg.cx, pg.T, pg.NT
    with contextlib.ExitStack() as st:
        qds = pg.sbn(st, 4, [128, 512], BF16, "qd")
        kds = pg.sbn(st, 4, [128, 512], BF16, "kd")
        vxs = pg.sbn(st, 6, [128, 8, 65], BF16, "vx")
        for vx, r_vx in vxs:
            cx.op("pool", lambda vx=vx: nc.gpsimd.memset(vx[:, :, 64:65], 1.0), [], [r_vx])
        qks = pg.sbn(st, 4, [64, 16, 128], BF16, "qkT")
        es = pg.sbn(st, 8, [128, 512], BF16, "e3")
        nds = pg.sbn(st, 2, [128, 8, 65], F32, "nd")
        m01, r_m01 = pg.sb(st, [128, 2, 256], BF16, "m01")
        tbd, r_tbd = pg.sb(st, [128, 256], BF16, "tbdil")
        cx.dma("sp", tbd[:], pg.c_in["tb_dil"], writes=[r_tbd])
        for j in range(2):
            pg.ts("dve", m01[:, j, :], tbd[:, :], -1.0, None, ALU.is_ge, None, [r_tbd], [r_m01])
        m01flat = m01[:, :, :].rearrange("p a t -> p (a t)")
        ectr = [0]

        def loads(k, g, r, c, ti):
            qv = pg.s_dq[g].rearrange("(i r) d -> r i d", r=r)
            kv = pg.s_dk[g].rearrange("(i r) d -> r i d", r=r)
            vv = pg.s_dv[g].rearrange("(i r) d -> r i d", r=r)
            rows = slice(ti * 128, (ti + 1) * 128)
            qd, r_qd = qds[k % 4]
            kd, r_kd = kds[k % 4]
            vx, r_vx = vxs[k % 6]
            cx.dma("sp", qd[:], qv[c, rows, :], writes=[r_qd])
            cx.dma("sp", kd[:], kv[c, rows, :], writes=[r_kd])
            cx.dma("sp", vx[:, :, 0:64], vv[c, rows, :].rearrange("p (h d) -> p h d", h=8), writes=[r_vx])

        def stage_a(k, g, r, c, ti):
            qd, r_qd = qds[k % 4]
            kd, r_kd = kds[k % 4]
            qk, r_qk = qks[k % 4]
            pqk, r_pqk = qks[(k - 1) % 4]
            for which, (src, r_src) in enumerate(((qd, r_qd), (kd, r_kd))):
                pst, r_ps = next_ps(pg)
                psb = pst[:, :].bitcast(BF16)
                for h in range(8):
                    pg.tr(psb[0:64, h * 128:(h + 1) * 128], src[:, h * 64:(h + 1) * 64], pg.ident_bf[:, :],
                          [r_src, pg.r_const], [r_ps])
                pg.cp("dve", qk[:, which * 8:(which + 1) * 8, :],
                      psb[0:64, 0:1024].rearrange("p (a t) -> p a t", a=8), [r_ps], [r_qk])
            has_prev = ti > 0
            elist = []
            for hp in range(4):
                pst, r_ps = next_ps(pg)
                first = True
                for hh in range(2):
                    h = hp * 2 + hh
                    pg.mm(pst[:, hh * 256:hh * 256 + 128], qk[:, 8 + h, :], qk[:, h, :], first, not has_prev,
                          [r_qk], [r_ps])
                    first = False
                    if has_prev:
                        pg.mm(pst[:, hh * 256 + 128:hh * 256 + 256], pqk[:, 8 + h, :], qk[:, h, :], False, True,
                              [r_qk, r_pqk], [r_ps])
                e, r_e = es[ectr[0] % 8]
                ectr[0] += 1
                if has_prev:
                    pg.act(e[:], pst[:, :], AF.Exp, [r_ps], [r_e], scale=0.125)
                    pg.tt("pool", e[:], e[:], m01flat, ALU.mult, [r_e, r_m01], [r_e])
                else:
                    for hh in range(2):
                        cs = slice(hh * 256, hh * 256 + 128)
                        pg.act(e[:, cs], pst[:, cs], AF.Exp, [r_ps], [r_e], scale=0.125)
                        pg.tt("pool", e[:, cs], e[:, cs], m01[:, 0, 0:128], ALU.mult, [r_e, r_m01], [r_e])
                elist.append((e, r_e))
            return (k, g, r, c, ti, elist)

        def stage_b(k, g, r, c, ti, elist):
            ov = pg.s_nd[g].rearrange("(i r) d -> r i d", r=r)
            rows = slice(ti * 128, (ti + 1) * 128)
            vx, r_vx = vxs[k % 6]
            pvx, r_pvx = vxs[(k - 1) % 6]
            nd, r_nd = nds[k % 2]
            has_prev = ti > 0
            pos = [pg.ps[6], pg.ps[7]]
            for hp in range(4):
                e, r_e = elist[hp]
                for hh in range(2):
                    h = hp * 2 + hh
                    po, r_po = pos[h // 4]
                    oc = slice((h % 4) * 65, (h % 4) * 65 + 65)
                    pg.mm(po[:, oc], e[:, hh * 256:hh * 256 + 128], vx[:, h, :], (h % 4 == 0), not has_prev,
                          [r_e, r_vx], [r_po])
                    if has_prev:
                        pg.mm(po[:, oc], e[:, hh * 256 + 128:hh * 256 + 256], pvx[:, h, :], False, True,
                              [r_e, r_pvx], [r_po])
            for half in range(2):
                po, r_po = pos[half]
                pg.cp("act" if half else "dve", nd[:, half * 4:(half + 1) * 4, :],
                      po[:, 0:260].rearrange("p (h d) -> p h d", d=65), [r_po], [r_nd])
            cx.dma("sp", ov[c, rows, :], nd[:, :, :].rearrange("p h d -> p (h d)"), reads=[r_nd])

        tiles = []
        for g, r in enumerate(DIL_RATES):
            nts = (T // r) // 128
            for c in range(r):
                for ti in range(nts):
                    tiles.append((g, r, c, ti))
        pend = None
        for k in range(min(3, len(tiles))):
            loads(k, *tiles[k])
        for k, (g, r, c, ti) in enumerate(tiles):
            if k + 3 < len(tiles):
                loads(k + 3, *tiles[k + 3])
            cur = stage_a(k, g, r, c, ti)
            if pend is not None:
                stage_b(*pend)
            pend = cur
        stage_b(*pend)
        cx.barrier()
        ins = [pg.sbn(st, 4, [128, 8, 65], F32, f"ndin{g}") for g in range(3)]
        recs = pg.sbn(st, 2, [128, 8], F32, "rec3")
        ocs = pg.sbn(st, 2, [128, 8, 64], BF16, "oc")

        def cload(i):
            rows = slice(i * 128, (i + 1) * 128)
            for g in range(3):
                tt_, rr_ = ins[g][i % 4]
                cx.dma("sp", tt_[:, :, :].rearrange("p h d -> p (h d)"), pg.s_nd[g][rows, :], writes=[rr_])
        for i in range(min(3, NT)):
            cload(i)
        for i in range(NT):
            rows = slice(i * 128, (i + 1) * 128)
            if i + 3 < NT:
                cload(i + 3)
            t = [ins[g][i % 4] for g in range(3)]
            pg.tt("dve", t[0][0][:], t[0][0][:], t[1][0][:], ALU.add, [t[0][1], t[1][1]], [t[0][1]])
            pg.tt("dve", t[0][0][:], t[0][0][:], t[2][0][:], ALU.add, [t[0][1], t[2][1]], [t[0][1]])
            rec, r_rec = recs[i % 2]
            oc, r_oc = ocs[i % 2]
            cx.op("dve", lambda: nc.vector.reciprocal(out=rec[:], in_=t[0][0][:, :, 64]), [t[0][1]], [r_rec])
            pg.tt("pool", oc[:], t[0][0][:, :, 0:64], rec[:, :].unsqueeze(2).to_broadcast([128, 8, 64]), ALU.mult,
                  [t[0][1], r_rec], [r_oc])
            cx.dma("sp", pg.s_oc[rows, :], oc[:, :, :].rearrange("p h d -> p (h d)"), reads=[r_oc])
        cx.barrier()


def phase4(pg, l):
    nc, cx, T, NT = pg.nc, pg.cx, pg.T, pg.NT
    W = pg.w
    pg.ps_pool = 5
    with contextlib.ExitStack() as st:
        def bc_load(src_ap, n, nm):
            t, r = pg.sb(st, [128, n], F32, nm)
            cx.dma("sp", t[:], src_ap.partition_broadcast(128), writes=[r])
            return t, r
        pg.oneh, r_c4 = pg.sb(st, [16, 16, 128], F32, "oneh")
        pg.ones_f, _ = pg.sb(st, [128, 128], F32, "onesf")
        cx.dma("sp", pg.oneh[:, :, :].rearrange("p a b -> p (a b)"), pg.c_in["oneh"], writes=[r_c4])
        cx.dma("sp", pg.ones_f[:], pg.c_in["ones_f"], writes=[r_c4])
        a_bc, r_a = bc_load(W["ssm_a_log"][l], 16, "abc")
        pg.act(a_bc[:], a_bc[:], AF.Exp, [r_a], [r_a])
        pg.ts("dve", a_bc[:], a_bc[:], -1.0, None, ALU.mult, None, [r_a], [r_a])
        d_bc, r_d = bc_load(W["ssm_d"][l], 16, "dbc")
        gn_bc, r_gn = bc_load(W["ssm_out_norm"][l], D, "gnbc")
        state, _ = pg.sb(st, [128, 16, 64], F32, "state")
        state_bf, _ = pg.sb(st, [128, 16, 64], BF16, "statebf")
        r_st = [R(), R()]
        r_stb = [R(), R()]
        for hf in range(2):
            hsl = slice(hf * 8, hf * 8 + 8)
            cx.op("pool", lambda: nc.gpsimd.memset(state[:, hsl, :], 0.0), [], [r_st[hf]])
            cx.op("pool", lambda: nc.gpsimd.memset(state_bf[:, hsl, :], 0.0), [], [r_stb[hf]])
        NBUF = 3
        xsTs = pg.sbn(st, 4, [128, 8, 128], F32, "xsTt")
        bcTs = pg.sbn(st, 4, [128, 8, 128], BF16, "bcTt")
        szs = pg.sbn(st, 4, [128, D], F32, "szt")

        def loads(i):
            tok = slice(i * 128, (i + 1) * 128)
            cx.dma("sp", xsTs[i % 4][0][:], pg.s_xsT[:, tok].rearrange("(c p) t -> p c t", p=128), writes=[xsTs[i % 4][1]])
            cx.dma("sp", bcTs[i % 4][0][:], pg.s_bcT[:, tok].rearrange("(c p) t -> p c t", p=128), writes=[bcTs[i % 4][1]])
            cx.dma("sp", szs[i % 4][0][:], pg.s_sz[tok, :], writes=[szs[i % 4][1]])
        xtms = pg.sbn(st, NBUF, [128, 16, 64], F32, "xtm")
        btms = pg.sbn(st, NBUF, [128, 512], BF16, "btm4")
        adts = pg.sbn(st, NBUF, [128, 16], F32, "adt")
        acss = pg.sbn(st, NBUF, [128, 16], F32, "acs")
        eacss = pg.sbn(st, NBUF, [128, 16], F32, "eacs")
        dsts = pg.sbn(st, NBUF, [128, 16], F32, "dst")
        cds = pg.sbn(st, NBUF, [128, 16], F32, "cd")
        acsTs = pg.sbn(st, NBUF, [16, 128], F32, "acsT")
        xdts = pg.sbn(st, NBUF, [128, 16, 64], BF16, "xdt")
        xdtds = pg.sbn(st, NBUF, [128, 16, 64], BF16, "xdtd")
        cbms = pg.sbn(st, 2, [128, 128], F32, "cbm")
        dds = pg.sbn(st, 2, [128, 4, 128], F32, "dd")
        mts = pg.sbn(st, NBUF * 4, [128, 4, 128], BF16, "mt")
        ysbs = pg.sbn(st, 2, [128, 16, 64], F32, "ysb")
        r_ys = [[R(), R()], [R(), R()]]
        xds = pg.sbn(st, NBUF, [128, 16, 64], F32, "xd")
        junks = pg.sbn(st, 2, [128, D], BF16, "junk4")
        sss = pg.sbn(st, 2, [128, 1], F32, "ss4")
        obs = pg.sbn(st, 2, [128, D], BF16, "ob4")
        kkc = [0]

        def stage_a(i):
            tok = slice(i * 128, (i + 1) * 128)
            b = i % NBUF
            xsT_t, r_xsT = xsTs[i % 4]
            bcT_t, r_bcT = bcTs[i % 4]
            x_tm, r_xtm = xtms[b]
            B_tm, r_btm = btms[b]
            adt, r_adt = adts[b]
            acs, r_acs = acss[b]
            eacs, r_eacs = eacss[b]
            dst, r_dst = dsts[b]
            cd, r_cd = cds[b]
            acsT, r_acsT = acsTs[b]
            xdt, r_xdt = xdts[b]
            xdtd, r_xdtd = xdtds[b]
            xd, r_xd = xds[b]
            for half in range(2):
                pst, r_ps = next_ps(pg)
                for c in range(4):
                    pg.tr(pst[:, c * 128:(c + 1) * 128], xsT_t[:, half * 4 + c, :], pg.ident_f[:, :],
                          [r_xsT, pg.r_const], [r_ps])
                pg.cp("act", x_tm[:, half * 8:(half + 1) * 8, :], pst[:, :].rearrange("p (h d) -> p h d", d=64),
                      [r_ps], [r_xtm])
            pst, r_ps = next_ps(pg)
            psb = pst[:, :].bitcast(BF16)
            for g in range(4):
                pg.tr(psb[:, g * 128:(g + 1) * 128], bcT_t[:, g, :], pg.ident_bf[:, :], [r_bcT, pg.r_const], [r_ps])
            pg.cp("act", B_tm[:], psb[:, 0:512], [r_ps], [r_btm])
            pg.tt("dve", adt[:], pg.dt_all[:, i, :], a_bc[:], ALU.mult, [pg.r_dt, r_a], [r_adt])
            ps1, r_ps1 = next_ps(pg)
            pg.mm(ps1[:, 0:16], pg.tri_f[:, :], adt[:], True, False, [r_adt, pg.r_const], [r_ps1])
            pg.mm(ps1[:, 16:32], pg.ones_f[:, :], adt[:], False, True, [r_adt, r_c4], [r_ps1])
            ps2, r_ps2 = next_ps(pg)
            pg.mm(ps2[0:16, 0:128], adt[:], pg.tri_f[:, :], True, True, [r_adt, pg.r_const], [r_ps2])
            pg.cp("dve", acs[:], ps1[:, 0:16], [r_ps1], [r_acs])
            pg.act(eacs[:], ps1[:, 0:16], AF.Exp, [r_ps1], [r_eacs])
            pg.act(cd[:], ps1[:, 16:32], AF.Exp, [r_ps1], [r_cd])
            pg.tt("dve", dst[:], ps1[:, 16:32], acs[:], ALU.subtract, [r_ps1, r_acs], [r_dst])
            pg.act(dst[:], dst[:], AF.Exp, [r_dst], [r_dst])
            pg.cp("act", acsT[:], ps2[0:16, 0:128], [r_ps2], [r_acsT])
            pg.tt("dve", xdt[:], x_tm[:], pg.dt_all[:, i, :].unsqueeze(2).to_broadcast([128, 16, 64]), ALU.mult,
                  [r_xtm, pg.r_dt], [r_xdt])
            pg.tt("pool", xdtd[:], xdt[:], dst[:, :].unsqueeze(2).to_broadcast([128, 16, 64]), ALU.mult,
                  [r_xdt, r_dst], [r_xdtd])
            pg.tt("pool", xd[:], x_tm[:], d_bc[:, :].unsqueeze(2).to_broadcast([128, 16, 64]), ALU.mult,
                  [r_xtm, r_d], [r_xd])
            for g in range(4):
                cbm, r_cbm = cbms[kkc[0] % 2]
                dd, r_dd = dds[kkc[0] % 2]
                kkc[0] += 1
                mt, r_mt = mts[b * 4 + g]
                pcb, r_pcb = next_ps(pg)
                pg.mm(pcb[:, 0:128], bcT_t[:, g, :], bcT_t[:, 4 + g, :], True, True, [r_bcT], [r_pcb])
                pg.tt("dve", cbm[:], pcb[:, 0:128], pg.tri_f[:, :], ALU.mult, [r_pcb, pg.r_const], [r_cbm])
                pd, r_pd = next_ps(pg)
                for j in range(4):
                    h = 4 * g + j
                    pg.mm(pd[:, j * 128:(j + 1) * 128], pg.oneh[:, h, :], acsT[:, :], j == 0, j == 3,
                          [r_acsT, r_c4], [r_pd])
                for j in range(4):
                    h = 4 * g + j
                    pg.ts("dve", dd[:, j, :], pd[:, j * 128:(j + 1) * 128], acs[:, h:h + 1], 0.0,
                          ALU.subtract, ALU.min, [r_pd, r_acs], [r_dd])
                pg.act(dd[:], dd[:], AF.Exp, [r_dd], [r_dd])
                pg.tt("pool", mt[:], dd[:], cbm[:, :].unsqueeze(1).to_broadcast([128, 4, 128]), ALU.mult,
                      [r_dd, r_cbm], [r_mt])

        def stage_b(i):
            tok = slice(i * 128, (i + 1) * 128)
            b = i % NBUF
            bcT_t, r_bcT = bcTs[i % 4]
            sz_t, r_sz = szs[i % 4]
            B_tm, r_btm = btms[b]
            eacs, r_eacs = eacss[b]
            cd, r_cd = cds[b]
            xdt, r_xdt = xdts[b]
            xdtd, r_xdtd = xdtds[b]
            xd, r_xd = xds[b]
            y_sb, _ = ysbs[i % 2]
            r_y = r_ys[i % 2]
            for hf in range(2):
                p_y, r_py = pg.ps[5]
                p_off, r_poff = pg.ps[6]
                p_st, r_pst = pg.ps[7]
                hsl = slice(hf * 8, hf * 8 + 8)
                for gg in range(2):
                    g = hf * 2 + gg
                    mt, r_mt = mts[b * 4 + g]
                    for j in range(4):
                        h = 4 * g + j
                        hl = h - hf * 8
                        pg.mm(p_y[:, hl * 64:(hl + 1) * 64], mt[:, j, :], xdt[:, h, :], (gg == 0 and j == 0),
                              (gg == 1 and j == 3), [r_mt, r_xdt], [r_py])
                    pg.mm(p_off[:, gg * 256:(gg + 1) * 256], bcT_t[:, 4 + g, :],
                          state_bf[:, 4 * g:4 * g + 4, :].rearrange("p h d -> p (h d)"), gg == 0, gg == 1,
                          [r_bcT, r_stb[hf]], [r_poff])
                    pg.mm(p_st[:, gg * 256:(gg + 1) * 256], B_tm[:, g * 128:(g + 1) * 128],
                          xdtd[:, 4 * g:4 * g + 4, :].rearrange("p h d -> p (h d)"), gg == 0, gg == 1,
                          [r_btm, r_xdtd], [r_pst])
                pg.tt("dve", y_sb[:, hsl, :], p_off[:, :].rearrange("p (h d) -> p h d", d=64),
                      eacs[:, hsl].unsqueeze(2).to_broadcast([128, 8, 64]), ALU.mult, [r_poff, r_eacs], [r_y[hf]])
                pg.tt("dve", y_sb[:, hsl, :], y_sb[:, hsl, :], p_y[:, :].rearrange("p (h d) -> p h d", d=64), ALU.add,
                      [r_y[hf], r_py], [r_y[hf]])
                pg.tt("pool", state[:, hsl, :], state[:, hsl, :], cd[:, hsl].unsqueeze(2).to_broadcast([128, 8, 64]),
                      ALU.mult, [r_st[hf], r_cd], [r_st[hf]])
                pg.tt("dve", state[:, hsl, :], state[:, hsl, :], p_st[:, :].rearrange("p (h d) -> p h d", d=64), ALU.add,
                      [r_st[hf], r_pst], [r_st[hf]])
                pg.cp("act", state_bf[:, hsl, :], state[:, hsl, :], [r_st[hf]], [r_stb[hf]])
            yf = y_sb[:, :, :].rearrange("p h d -> p (h d)")
            pg.tt("pool", yf, yf, xd[:, :, :].rearrange("p h d -> p (h d)"), ALU.add, [r_y[0], r_y[1], r_xd], [r_y[0], r_y[1]])
            pg.tt("dve", yf, yf, sz_t[:], ALU.mult, [r_y[0], r_y[1], r_sz], [r_y[0], r_y[1]])
            jk, r_jk = junks[i % 2]
            ss, r_ss = sss[i % 2]
            ob, r_ob = obs[i % 2]
            pg.act(jk[:], yf, AF.Square, [r_y[0], r_y[1]], [r_jk, r_ss], accum_out=ss[:, 0:1])
            pg.ts("dve", ss[:], ss[:], 1.0 / D, EPS, ALU.mult, ALU.add, [r_ss], [r_ss])
            pg.act(ss[:], ss[:], AF.Sqrt, [r_ss], [r_ss])
            cx.op("dve", lambda: nc.vector.reciprocal(out=ss[:], in_=ss[:]), [r_ss], [r_ss])
            pg.ts("dve", yf, yf, ss[:, 0:1], None, ALU.mult, None, [r_y[0], r_y[1], r_ss], [r_y[0], r_y[1]])
            pg.tt("pool", ob[:], yf, gn_bc[:], ALU.mult, [r_y[0], r_y[1], r_gn], [r_ob])
            cx.dma("sp", pg.s_ob[tok, :], ob[:], reads=[r_ob])

        for i in range(min(3, NT)):
            loads(i)
        stage_a(0)
        for i in range(NT):
            if i + 3 < NT:
                loads(i + 3)
            if i + 1 < NT:
                stage_a(i + 1)
            stage_b(i)
        cx.barrier()
    pg.ps_pool = 6


def load_weight(pg, st, w_ap, K, N, name, cchunk=1024):
    kc = K // 128
    t, r = pg.sb(st, [128, kc, N], BF16, name)
    wv = w_ap.rearrange("(kc p) n -> p kc n", p=128)
    for c0 in range(0, N, cchunk):
        c1 = min(N, c0 + cchunk)
        pg.cx.dma("pool", t[:, :, c0:c1], wv[:, :, c0:c1], writes=[r])
    return t, r


def transpose_tile(pg, src, r_src, nblk, dst, r_dst, ek="dve"):
    for b0 in range(0, nblk, 8):
        n = min(8, nblk - b0)
        pst, r_ps = next_ps(pg)
        psb = pst[:, :].bitcast(BF16)
        for k in range(n):
            pg.tr(psb[:, k * 128:(k + 1) * 128], src[:, (b0 + k) * 128:(b0 + k + 1) * 128], pg.ident_bf[:, :],
                  [r_src, pg.r_const], [r_ps])
        pg.cp(ek, dst[:, b0:b0 + n, :], psb[:, 0:n * 128].rearrange("p (k t) -> p k t", k=n), [r_ps], [r_dst])


def phase5(pg, l, outer):
    nc, cx, T, NT = pg.nc, pg.cx, pg.T, pg.NT
    W = pg.w
    x_src = pg.x_in if l == 0 else pg.xres
    wup_t, wup_r = pg.sb(outer, [128, 8, 2 * FFN], BF16, "wup")
    with contextlib.ExitStack() as st:
        wa, r_wa = load_weight(pg, st, W["w_br_moba"][l], 512, D, "wa")
        wm, r_wm = load_weight(pg, st, W["w_br_ssm"][l], D, D, "wm")
        wc, r_wc = load_weight(pg, st, W["w_br_dil"][l], 512, D, "wc")
        wo, r_wo = load_weight(pg, st, W["w_out"][l], D, D, "wo")
        wv = W["w_up"][l].rearrange("(kc p) n -> p kc n", p=128)
        for c0 in range(0, 2 * FFN, 1408):
            cx.dma("pool", wup_t[:, :, c0:c0 + 1408], wv[:, :, c0:c0 + 1408], writes=[wup_r])
        pg.wup_pre = (wup_t, wup_r)
        oas = pg.sbn(st, 3, [128, 512], BF16, "oat")
        obs_ = pg.sbn(st, 3, [128, D], BF16, "obt")
        ocs = pg.sbn(st, 3, [128, 512], BF16, "oct")
        gts = pg.sbn(st, 3, [128, 3 * D], BF16, "gt")
        xts = pg.sbn(st, 3, [128, D], F32, "xt5")
        oaTs = pg.sbn(st, 2, [128, 4, 128], BF16, "oaT")
        obTs = pg.sbn(st, 2, [128, 8, 128], BF16, "obT")
        ocTs = pg.sbn(st, 2, [128, 4, 128], BF16, "ocT")
        t1s = pg.sbn(st, 2, [128, 512], F32, "t1")
        t2s = pg.sbn(st, 2, [128, 512], F32, "t2")
        mgs = pg.sbn(st, 2, [128, D], BF16, "mg")
        mgTs = pg.sbn(st, 2, [128, 8, 128], BF16, "mgT")
        kkc = [0]

        def loads(i):
            tok = slice(i * 128, (i + 1) * 128)
            b4 = i % 3
            cx.dma("sp", oas[b4][0][:], pg.s_oa[tok, :], writes=[oas[b4][1]])
            cx.dma("sp", obs_[b4][0][:], pg.s_ob[tok, :], writes=[obs_[b4][1]])
            cx.dma("sp", ocs[b4][0][:], pg.s_oc[tok, :], writes=[ocs[b4][1]])
            cx.dma("sp", gts[b4][0][:], pg.s_gate[tok, :], writes=[gts[b4][1]])
            cx.dma("sp", xts[b4][0][:], x_src[tok, :], reads=([pg.r_xres[i]] if l > 0 else []), writes=[xts[b4][1]])

        def stage_a(i):
            tok = slice(i * 128, (i + 1) * 128)
            b = i % 2
            oa, r_oa = oas[i % 3]
            ob, r_ob = obs_[i % 3]
            oc, r_oc = ocs[i % 3]
            gt, r_gt = gts[i % 3]
            oaT, r_oaT = oaTs[b]
            obT, r_obT = obTs[b]
            ocT, r_ocT = ocTs[b]
            mg, r_mg = mgs[b]
            transpose_tile(pg, oa, r_oa, 4, oaT, r_oaT, "dve")
            transpose_tile(pg, ob, r_ob, 8, obT, r_obT, "act")
            transpose_tile(pg, oc, r_oc, 4, ocT, r_ocT, "dve")
            for half in range(2):
                cs = slice(half * 512, (half + 1) * 512)
                t1, r_t1 = t1s[kkc[0] % 2]
                t2, r_t2 = t2s[kkc[0] % 2]
                kkc[0] += 1
                pa, r_pa = next_ps(pg)
                for kc in range(4):
                    pg.mm(pa[:, :], oaT[:, kc, :], wa[:, kc, cs], kc == 0, kc == 3, [r_oaT, r_wa], [r_pa])
                pm, r_pm = next_ps(pg)
                for kc in range(8):
                    pg.mm(pm[:, :], obT[:, kc, :], wm[:, kc, cs], kc == 0, kc == 7, [r_obT, r_wm], [r_pm])
                pc, r_pc = next_ps(pg)
                for kc in range(4):
                    pg.mm(pc[:, :], ocT[:, kc, :], wc[:, kc, cs], kc == 0, kc == 3, [r_ocT, r_wc], [r_pc])
                pg.tt("dve", t1[:], pa[:, :], gt[:, half * 512:(half + 1) * 512], ALU.mult, [r_pa, r_gt], [r_t1])
                pg.tt("dve", t2[:], pm[:, :], gt[:, D + half * 512:D + (half + 1) * 512], ALU.mult, [r_pm, r_gt], [r_t2])
                pg.tt("pool", t1[:], t1[:], t2[:], ALU.add, [r_t1, r_t2], [r_t1])
                pg.tt("dve", t2[:], pc[:, :], gt[:, 2 * D + half * 512:2 * D + (half + 1) * 512], ALU.mult,
                      [r_pc, r_gt], [r_t2])
                pg.tt("pool", mg[:, cs], t1[:], t2[:], ALU.add, [r_t1, r_t2], [r_mg])

        def stage_b(i):
            tok = slice(i * 128, (i + 1) * 128)
            b = i % 2
            xt, r_xt = xts[i % 3]
            mg, r_mg = mgs[b]
            mgT, r_mgT = mgTs[b]
            transpose_tile(pg, mg, r_mg, 8, mgT, r_mgT, "act")
            for half in range(2):
                cs = slice(half * 512, (half + 1) * 512)
                po, r_po = next_ps(pg)
                for kc in range(8):
                    pg.mm(po[:, :], mgT[:, kc, :], wo[:, kc, cs], kc == 0, kc == 7, [r_mgT, r_wo], [r_po])
                pg.tt("dve", xt[:, cs], xt[:, cs], po[:, :], ALU.add, [r_xt, r_po], [r_xt])
            cx.dma("sp", pg.xres[tok, :], xt[:], reads=[r_xt], writes=[pg.r_xres[i]])

        for i in range(min(2, NT)):
            loads(i)
        stage_a(0)
        for i in range(NT):
            if i + 2 < NT:
                loads(i + 2)
            if i + 1 < NT:
                stage_a(i + 1)
            stage_b(i)
        cx.barrier()


def rms_tile(pg, xt, r_xt, xn, r_xn, jk, r_jk, ss, r_ss):
    nc = pg.nc
    pg.act(jk[:], xt[:], AF.Square, [r_xt], [r_jk, r_ss], accum_out=ss[:, 0:1])
    pg.ts("dve", ss[:], ss[:], 1.0 / D, EPS, ALU.mult, ALU.add, [r_ss], [r_ss])
    pg.act(ss[:], ss[:], AF.Sqrt, [r_ss], [r_ss])
    pg.cx.op("dve", lambda: nc.vector.reciprocal(out=ss[:], in_=ss[:]), [r_ss], [r_ss])
    pg.ts("dve", xn[:], xt[:], ss[:, 0:1], None, ALU.mult, None, [r_xt, r_ss], [r_xn])


def phase6(pg, l):
    nc, cx, T, NT = pg.nc, pg.cx, pg.T, pg.NT
    W = pg.w
    NJ = FFN // 128
    NG = T // 512
    with contextlib.ExitStack() as st:
        wup, r_wup = pg.wup_pre
        wdn, r_wdn = load_weight(pg, st, W["w_down"][l], FFN, D, "wdn")
        gT, r_g = pg.sb(st, [128, 8], F32, "gT6")
        cw, r_cw = pg.sb(st, [128, 2 * NJ, 3], F32, "cw6")
        cb, r_cb = pg.sb(st, [128, 2 * NJ], F32, "cb6")
        with nc.allow_non_contiguous_dma(reason="small per-layer vectors"):
            cx.dma("sp", gT[:], W["norm_ffn"][l].rearrange("(kc p) -> p kc", p=128), writes=[r_g])
            for k in range(3):
                cx.dma("sp", cw[:, :, k], W["ffn_conv_w"][l][k].rearrange("(cc p) -> p cc", p=128), writes=[r_cw])
            cx.dma("sp", cb[:], W["ffn_conv_b"][l].rearrange("(cc p) -> p cc", p=128), writes=[r_cb])
        halo, r_halo = pg.sb(st, [128, 2 * NJ, 2], F32, "halo")
        cx.op("pool", lambda: nc.gpsimd.memset(halo[:], 0.0), [], [r_halo])
        uTs = pg.sbn(st, 2, [128, 8, 512], BF16, "uT6")
        hTs = pg.sbn(st, 1, [128, NJ, 512], BF16, "hT")
        xts = pg.sbn(st, 2, [128, D], F32, "xt6")
        xcs = pg.sbn(st, 2, [128, D], F32, "xc6")
        xns = pg.sbn(st, 2, [128, D], BF16, "xn6")
        sss = pg.sbn(st, 2, [128, 1], F32, "ss6")
        raws = pg.sbn(st, 2, [128, 2 + 512], F32, "raw6")
        accs = pg.sbn(st, 3, [128, 512], F32, "acc6")
        kkc = [0]
        k2c = [0]

        def part_a(tg):
            uT, r_uT = uTs[tg % 2]

            def gen(ti):
                i = tg * 4 + ti
                xt, r_xt = xts[ti % 2]
                xn, r_xn = xns[ti % 2]
                ss, r_ss = sss[ti % 2]
                cx.dma("sp", xt[:], pg.xres[i * 128:(i + 1) * 128, :], reads=[pg.r_xres[i]], writes=[r_xt])
                yield
                pg.act(xn[:], xt[:], AF.Square, [r_xt], [r_xn, r_ss], accum_out=ss[:, 0:1])
                yield
                pg.ts("dve", ss[:], ss[:], 1.0 / D, EPS, ALU.mult, ALU.add, [r_ss], [r_ss])
                yield
                pg.act(ss[:], ss[:], AF.Sqrt, [r_ss], [r_ss])
                yield
                cx.op("dve", lambda: nc.vector.reciprocal(out=ss[:], in_=ss[:]), [r_ss], [r_ss])
                pg.ts("dve", xn[:], xt[:], ss[:, 0:1], None, ALU.mult, None, [r_xt, r_ss], [r_xn])
                yield
                pst, r_ps = next_ps(pg)
                psb = pst[:, :].bitcast(BF16)
                for kc in range(8):
                    pg.tr(psb[:, kc * 128:(kc + 1) * 128], xn[:, kc * 128:(kc + 1) * 128], pg.ident_bf[:, :],
                          [r_xn, pg.r_const], [r_ps])
                yield
                pg.tt("dve", uT[:, :, ti * 128:(ti + 1) * 128], psb[:, 0:1024].rearrange("p (k t) -> p k t", k=8),
                      gT[:, :].unsqueeze(2).to_broadcast([128, 8, 128]), ALU.mult, [r_ps, r_g], [r_uT])
            lockstep([gen(0), gen(1)])
            lockstep([gen(2), gen(3)])

        def part_b(tg):
            uT, r_uT = uTs[tg % 2]
            hT, r_hT = hTs[0]
            for j in range(NJ):
                accp = []
                for part in range(2):
                    ch = part * NJ + j
                    pst, r_ps = next_ps(pg)
                    for kc in range(8):
                        pg.mm(pst[:, :], wup[:, kc, ch * 128:(ch + 1) * 128], uT[:, kc, :], kc == 0, kc == 7,
                              [r_wup, r_uT], [r_ps])
                    raw, r_raw = raws[k2c[0] % 2]
                    acc, r_acc = accs[k2c[0] % 3]
                    k2c[0] += 1
                    pg.cp("pool", raw[:, 0:2], halo[:, ch, :], [r_halo], [r_raw])
                    pg.cp("act", raw[:, 2:514], pst[:, :], [r_ps], [r_raw])
                    pg.cp("pool", halo[:, ch, :], raw[:, 512:514], [r_raw], [r_halo])
                    pg.ts("dve", acc[:], raw[:, 0:512], cw[:, ch, 0:1], cb[:, ch:ch + 1], ALU.mult, ALU.add,
                          [r_raw, r_cw, r_cb], [r_acc])
                    for q in range(1, 3):
                        cx.op("dve", lambda q=q: nc.vector.scalar_tensor_tensor(
                            out=acc[:], in0=raw[:, q:q + 512], scalar=cw[:, ch, q:q + 1], in1=acc[:],
                            op0=ALU.mult, op1=ALU.add), [r_raw, r_cw, r_acc], [r_acc])
                    accp.append((acc, r_acc))
                pg.act(accp[0][0][:], accp[0][0][:], AF.Silu, [accp[0][1]], [accp[0][1]])
                pg.tt("pool", hT[:, j, :], accp[0][0][:], accp[1][0][:], ALU.mult, [accp[0][1], accp[1][1]], [r_hT])

        def part_c(tg):
            hT, r_hT = hTs[0]
            def ld(ti):
                i = tg * 4 + ti
                xt, r_xt = xcs[ti % 2]
                cx.dma("sp", xt[:], pg.xres[i * 128:(i + 1) * 128, :], reads=[pg.r_xres[i]], writes=[r_xt])
            ld(0)
            ld(1)
            for ti in range(4):
                i = tg * 4 + ti
                xt, r_xt = xcs[ti % 2]
                if ti >= 1 and ti + 1 < 4:
                    ld(ti + 1)
                for half in range(2):
                    cs = slice(half * 512, (half + 1) * 512)
                    po, r_po = next_ps(pg)
                    for j in range(NJ):
                        pg.mm(po[:, :], hT[:, j, ti * 128:(ti + 1) * 128], wdn[:, j, cs], j == 0, j == NJ - 1,
                              [r_hT, r_wdn], [r_po])
                    pg.tt("dve", xt[:, cs], xt[:, cs], po[:, :], ALU.add, [r_xt, r_po], [r_xt])
                cx.dma("sp", pg.xres[i * 128:(i + 1) * 128, :], xt[:], reads=[r_xt], writes=[pg.r_xres[i]])

        part_a(0)
        for tg in range(NG):
            part_b(tg)
            if tg + 1 < NG:
                part_a(tg + 1)
            part_c(tg)
        cx.barrier()


def phase7(pg, l):
    nc, cx, T, NT = pg.nc, pg.cx, pg.T, pg.NT
    W = pg.w
    with contextlib.ExitStack() as st:
        wpg, r_wpg = load_weight(pg, st, W["w_ple_gate"][l], D, D, "wpg")
        wpl, r_wpl = load_weight(pg, st, W["w_ple"][l], 256, D, "wpl")
        gT, r_g = pg.sb(st, [128, 8], F32, "gT7")
        with nc.allow_non_contiguous_dma(reason="small gamma load"):
            cx.dma("sp", gT[:], W["norm_ple"][l].rearrange("(kc p) -> p kc", p=128), writes=[r_g])
        NX = 6
        xts = pg.sbn(st, NX, [128, D], F32, "xt7")
        xns = pg.sbn(st, 4, [128, D], BF16, "xn7")
        sss = pg.sbn(st, 4, [128, 1], F32, "ss7")
        uTs = pg.sbn(st, 4, [128, 8, 128], BF16, "uT7")
        pfs = pg.sbn(st, NX, [128, 256], F32, "pf")
        pbs = pg.sbn(st, 4, [128, 256], BF16, "pb")
        pTs = pg.sbn(st, 4, [128, 2, 128], BF16, "pT")
        sgs = pg.sbn(st, 2, [128, 512], F32, "sg")
        t1s = pg.sbn(st, 2, [128, 512], F32, "t17")
        kkc = [0]

        def loads(i):
            tok = slice(i * 128, (i + 1) * 128)
            xt, r_xt = xts[i % NX]
            pf, r_pf = pfs[i % NX]
            cx.dma("sp", xt[:], pg.xres[tok, :], reads=[pg.r_xres[i]], writes=[r_xt])
            cx.dma("sp", pf[:], pg.p_in[l][tok, :], writes=[r_pf])

        def stage_a(i):
            b = i % 4
            xt, r_xt = xts[i % NX]
            xn, r_xn = xns[b]
            ss, r_ss = sss[b]
            uT, r_uT = uTs[b]
            pf, r_pf = pfs[i % NX]
            pb, r_pb = pbs[b]
            pT, r_pT = pTs[b]
            pg.act(xn[:], xt[:], AF.Square, [r_xt], [r_xn, r_ss], accum_out=ss[:, 0:1])
            pg.cp("pool", pb[:], pf[:], [r_pf], [r_pb])
            yield
            pg.ts("dve", ss[:], ss[:], 1.0 / D, EPS, ALU.mult, ALU.add, [r_ss], [r_ss])
            yield
            pg.act(ss[:], ss[:], AF.Sqrt, [r_ss], [r_ss])
            yield
            cx.op("dve", lambda: nc.vector.reciprocal(out=ss[:], in_=ss[:]), [r_ss], [r_ss])
            pg.ts("dve", xn[:], xt[:], ss[:, 0:1], None, ALU.mult, None, [r_xt, r_ss], [r_xn])
            yield
            pst, r_ps = next_ps(pg)
            psb = pst[:, :].bitcast(BF16)
            for kc in range(8):
                pg.tr(psb[:, kc * 128:(kc + 1) * 128], xn[:, kc * 128:(kc + 1) * 128], pg.ident_bf[:, :],
                      [r_xn, pg.r_const], [r_ps])
            yield
            pg.tt("dve", uT[:, :, :], psb[:, 0:1024].rearrange("p (k t) -> p k t", k=8),
                  gT[:, :].unsqueeze(2).to_broadcast([128, 8, 128]), ALU.mult, [r_ps, r_g], [r_uT])
            transpose_tile(pg, pb, r_pb, 2, pT, r_pT, "act")

        def stage_b(i):
            tok = slice(i * 128, (i + 1) * 128)
            b = i % 4
            xt, r_xt = xts[i % NX]
            uT, r_uT = uTs[b]
            pT, r_pT = pTs[b]
            for half in range(2):
                cs = slice(half * 512, (half + 1) * 512)
                sg, r_sg = sgs[kkc[0] % 2]
                t1, r_t1 = t1s[kkc[0] % 2]
                kkc[0] += 1
                pgt, r_pgt = next_ps(pg)
                for kc in range(8):
                    pg.mm(pgt[:, :], uT[:, kc, :], wpg[:, kc, cs], kc == 0, kc == 7, [r_uT, r_wpg], [r_pgt])
                ppe, r_ppe = next_ps(pg)
                for kc in range(2):
                    pg.mm(ppe[:, :], pT[:, kc, :], wpl[:, kc, cs], kc == 0, kc == 1, [r_pT, r_wpl], [r_ppe])
                pg.act(sg[:], pgt[:, :], AF.Sigmoid, [r_pgt], [r_sg])
                pg.tt("dve", t1[:], ppe[:, :], sg[:], ALU.mult, [r_ppe, r_sg], [r_t1])
                pg.tt("pool", xt[:, cs], xt[:, cs], t1[:], ALU.add, [r_xt, r_t1], [r_xt])
            cx.dma("sp", pg.xres[tok, :], xt[:], reads=[r_xt], writes=[pg.r_xres[i]])

        NP = NT // 2
        for i in range(min(4, NT)):
            loads(i)
        lockstep([stage_a(0), stage_a(1)])
        for p in range(NP):
            for i in (2 * p + 4, 2 * p + 5):
                if i < NT:
                    loads(i)
            if p + 1 < NP:
                lockstep([stage_a(2 * p + 2), stage_a(2 * p + 3)])
            stage_b(2 * p)
            stage_b(2 * p + 1)
        cx.barrier()


def build(T, depth, debug=False, upto=99):
    pg = Prog(T, depth, debug)
    setup(pg)
    rope_tables(pg)
    for l in range(depth):
        phase1(pg, l)
        if upto <= 1:
            break
        phase2(pg, l)
        if upto <= 2:
            break
        phase3(pg, l)
        if upto <= 3:
            break
        phase4(pg, l)
        if upto <= 4:
            break
        with contextlib.ExitStack() as st56:
            phase5(pg, l, st56)
            phase6(pg, l)
            pg.cx.barrier()
        phase7(pg, l)
    pg.cx.finish()
    return pg


def make_in_maps(inputs, T, depth, n_cores):
    hc = host_consts(T)
    maps = []
    for c in range(n_cores):
        b = c % inputs["x"].shape[0]
        m = {"x": np.ascontiguousarray(inputs["x"][b, :T]),
             "p": np.ascontiguousarray(inputs["p"][:depth, b, :T]),
             "positions": np.ascontiguousarray(inputs["positions"][b, :T]).astype(np.int32)}
        for k, v in inputs.items():
            if k in ("x", "p", "positions"):
                continue
            m[k] = np.ascontiguousarray(v[:depth])
        for k, v in hc.items():
            m["c_" + k] = v
        maps.append(m)
    return maps


_CACHE = {}


def kernel(**inputs):
    T, depth, ncores = 4096, 4, 8
    inputs = {k: np.asarray(v) for k, v in inputs.items()}
    if "pg" not in _CACHE:
        _CACHE["pg"] = build(T, depth)
    pg = _CACHE["pg"]
    maps = make_in_maps(inputs, T, depth, ncores)
    res = run_bass_kernel_spmd(pg.nc, maps, core_ids=list(range(ncores)))
    B = inputs["x"].shape[0]
    out = np.stack([np.asarray(res.results[b]["y"], dtype=np.float32) for b in range(B)], 0)
    return out
```
